# Optimizing a Trainium2 kernel written in Bass

```python
import jax, jax.numpy as jnp
from jax import lax
import numpy as np

D_MODEL = 1024
BATCH = 16
SEQ = 4096
DEPTH = 2

N_MIXERS = 2
RMS_EPS = 1e-6
GM_CHUNK = 128
GM_WIDTH = 2 * D_MODEL
GM_GROUPS = 8
GM_GROUP_DIM = GM_WIDTH // GM_GROUPS
RET_QK_DIM = 256
RET_HEADS = D_MODEL // RET_QK_DIM
RET_V_DIM = 2 * D_MODEL // RET_HEADS
RET_CHUNK = 128
ROPE_BASE = 10000.0
D_FF = 4 * D_MODEL
N_A = (DEPTH + 1) // 2
N_B = DEPTH // 2

kernel_name = "hybrid_gmlp_retention_trunk"


def rmsnorm(x, g):
    xf = x.astype(jnp.float32)
    y = xf * lax.rsqrt(jnp.mean(xf * xf, axis=-1, keepdims=True) + RMS_EPS)
    return (y * g.astype(jnp.float32)).astype(x.dtype)


def chunked_spatial_gating(xn, w_in, b_in, v_norm_g, w_s, b_s, w_out):
    B, S, _ = xn.shape
    nc = S // GM_CHUNK
    z = jax.nn.gelu(xn @ w_in + b_in)
    u, v = jnp.split(z, 2, axis=-1)
    v = rmsnorm(v, v_norm_g)
    v = v.reshape(B, nc, GM_CHUNK, GM_GROUPS, GM_GROUP_DIM)
    causal = jnp.tril(jnp.ones((GM_CHUNK, GM_CHUNK), dtype=bool))
    w = jnp.where(causal[None], w_s, 0.0)
    s = jnp.einsum('gts,bnsgc->bntgc', w, v) + b_s.T[:, :, None]
    s = s.reshape(B, S, GM_WIDTH)
    return (u * s) @ w_out


def rotary(t, cos, sin):
    t1, t2 = jnp.split(t, 2, axis=-1)
    return jnp.concatenate([t1 * cos - t2 * sin, t2 * cos + t1 * sin], axis=-1)


def retention(xn, w_in, head_norm_g, w_out):
    B, S, _ = xn.shape
    H, dk, dv, C = RET_HEADS, RET_QK_DIM, RET_V_DIM, RET_CHUNK
    nc = S // C
    f32 = jnp.float32
    proj = xn @ w_in
    q, k, v, g = jnp.split(proj, [H * dk, 2 * H * dk, 2 * H * dk + H * dv], axis=-1)
    q = q.reshape(B, S, H, dk).astype(f32)
    k = k.reshape(B, S, H, dk).astype(f32) * (dk ** -0.5)
    v = v.reshape(B, S, H, dv).astype(f32)
    pos = jnp.arange(S, dtype=f32)
    inv_freq = 1.0 / (ROPE_BASE ** jnp.linspace(0.0, 1.0, dk // 2, dtype=f32))
    ang = pos[:, None] * inv_freq[None, :]
    cos = jnp.cos(ang)[None, :, None, :]
    sin = jnp.sin(ang)[None, :, None, :]
    q = rotary(q, cos, sin)
    k = rotary(k, cos, sin)
    log_gamma = jnp.log(1.0 - 2.0 ** (-5.0 - jnp.arange(H, dtype=f32)))
    idx = jnp.arange(C, dtype=f32)
    diff = idx[:, None] - idx[None, :]
    inner_decay = jnp.where(diff >= 0, jnp.exp(log_gamma[:, None, None] * jnp.maximum(diff, 0.0)), 0.0)
    query_decay = jnp.exp(log_gamma[:, None] * (idx + 1.0))[None, :, :, None]
    key_decay = jnp.exp(log_gamma[:, None] * (C - 1.0 - idx))[None, :, :, None]
    chunk_decay = jnp.exp(log_gamma * C)[None, :, None, None]

    def to_chunks(t):
        return t.reshape(B, nc, C, H, t.shape[-1]).transpose(1, 0, 3, 2, 4)

    def step(state, qkv):
        qi, ki, vi = qkv
        scores = jnp.einsum('bhik,bhjk->bhij', qi, ki) * inner_decay
        o = jnp.einsum('bhij,bhjv->bhiv', scores, vi)
        o = o + jnp.einsum('bhik,bhkv->bhiv', qi * query_decay, state)
        state = state * chunk_decay + jnp.einsum('bhjk,bhjv->bhkv', ki * key_decay, vi)
        return state, o

    state0 = jnp.zeros((B, H, dk, dv), f32)
    _, o = lax.scan(step, state0, (to_chunks(q), to_chunks(k), to_chunks(v)))
    o = o.transpose(1, 0, 3, 2, 4).reshape(B, S, H, dv)
    o = rmsnorm(o, head_norm_g.reshape(H, dv)).reshape(B, S, H * dv)
    o = (o * jax.nn.silu(g.astype(f32))).astype(xn.dtype)
    return o @ w_out


def squared_relu_mlp(h, w1, w2):
    a = jax.nn.relu(h @ w1)
    return (a * a) @ w2


def setup_inputs(seed: int = 0) -> dict:
    key = jax.random.key(seed)
    ks = jax.random.split(key, 16)
    f32 = jnp.float32
    nrm = lambda k, shape, scale: jax.random.normal(k, shape, f32) * scale
    return {
        "x": nrm(ks[0], (BATCH, SEQ, D_MODEL), 1.0),
        "norm_mix_g": 1.0 + nrm(ks[1], (DEPTH, D_MODEL), 0.02),
        "norm_ffn_g": 1.0 + nrm(ks[2], (DEPTH, D_MODEL), 0.02),
        "a_w_in": nrm(ks[3], (N_A, D_MODEL, 2 * GM_WIDTH), D_MODEL ** -0.5),
        "a_b_in": nrm(ks[4], (N_A, 2 * GM_WIDTH), 0.02),
        "a_v_norm_g": 1.0 + nrm(ks[5], (N_A, GM_WIDTH), 0.02),
        "a_w_s": nrm(ks[6], (N_A, GM_GROUPS, GM_CHUNK, GM_CHUNK), GM_CHUNK ** -0.5),
        "a_b_s": 1.0 + nrm(ks[7], (N_A, GM_GROUPS, GM_CHUNK), 0.02),
        "a_w_out": nrm(ks[8], (N_A, GM_WIDTH, D_MODEL), GM_WIDTH ** -0.5),
        "b_w_in": nrm(ks[9], (N_B, D_MODEL, 2 * RET_HEADS * (RET_QK_DIM + RET_V_DIM)), D_MODEL ** -0.5),
        "b_head_norm_g": 1.0 + nrm(ks[10], (N_B, RET_HEADS * RET_V_DIM), 0.02),
        "b_w_out": nrm(ks[11], (N_B, RET_HEADS * RET_V_DIM, D_MODEL), (RET_HEADS * RET_V_DIM) ** -0.5),
        "mlp_w1": nrm(ks[12], (DEPTH, D_MODEL, D_FF), D_MODEL ** -0.5),
        "mlp_w2": nrm(ks[13], (DEPTH, D_FF, D_MODEL), D_FF ** -0.5),
        "final_norm_g": 1.0 + nrm(ks[14], (D_MODEL,), 0.02),
    }


def reference(x, norm_mix_g, norm_ffn_g, a_w_in, a_b_in, a_v_norm_g, a_w_s, a_b_s, a_w_out,
              b_w_in, b_head_norm_g, b_w_out, mlp_w1, mlp_w2, final_norm_g):
    for i in range(DEPTH):
        h = rmsnorm(x, norm_mix_g[i])
        j = i // N_MIXERS
        if i % N_MIXERS == 0:
            h = chunked_spatial_gating(h, a_w_in[j], a_b_in[j], a_v_norm_g[j], a_w_s[j], a_b_s[j], a_w_out[j])
        else:
            h = retention(h, b_w_in[j], b_head_norm_g[j], b_w_out[j])
        x = x + h
        h = rmsnorm(x, norm_ffn_g[i])
        x = x + squared_relu_mlp(h, mlp_w1[i], mlp_w2[i])
    return rmsnorm(x, final_norm_g)
```

```python
import numpy as np
from contextlib import ExitStack
import concourse.bass as bass
import concourse.mybir as mybir
from concourse.bass_utils import run_bass_kernel_spmd

F32 = mybir.dt.float32
BF16 = mybir.dt.bfloat16
ALU = mybir.AluOpType
AF = mybir.ActivationFunctionType

D = 1024
T = 512
NCH = 4
NSLOT = 4
NTAB = 2
ARENA_KIB = 88
EPS = 1e-6


class Buf:
    __slots__ = ("name", "lw", "rd")

    def __init__(self, name):
        self.name = name
        self.lw = None
        self.rd = []


class DSem:
    def __init__(self, sem, group=False):
        self.sem = sem
        self.total = 0
        self.group = group


class Op:
    __slots__ = ("eng", "fn", "deps", "marked", "token", "dsem", "idx", "stage")


class V:
    __slots__ = ("ap", "bufs")

    def __init__(self, ap, bufs):
        self.ap = ap
        self.bufs = bufs


class Prog:
    ENGS = ["pe", "act", "dve", "pool", "sp"]

    def __init__(self):
        self.ops = []
        self.stage = "setup"

    def add(self, eng, fn, reads=(), writes=(), dsem=None):
        op = Op()
        op.eng = eng
        op.fn = fn
        op.dsem = dsem
        op.idx = len(self.ops)
        op.marked = False
        op.token = None
        op.stage = self.stage
        deps = set()
        for b in reads:
            if b.lw is not None:
                deps.add(b.lw)
        for b in writes:
            if b.lw is not None:
                deps.add(b.lw)
            deps.update(b.rd)
        for b in writes:
            b.lw = op.idx
            b.rd = []
        for b in reads:
            if b.lw != op.idx:
                b.rd.append(op.idx)
        deps.discard(op.idx)
        op.deps = deps
        self.ops.append(op)
        return op

    def emit(self, nc, block, esem):
        ops = self.ops
        for op in ops:
            for d in op.deps:
                dop = ops[d]
                if dop.dsem is None:
                    if dop.eng == "pe" and op.eng == "pe" and op.dsem is None:
                        continue
                    dop.marked = True
        cnt = {e: 0 for e in self.ENGS}
        for op in ops:
            if op.dsem is not None:
                op.dsem.total += 16
                op.token = (op.dsem.sem, op.dsem.total)
            elif op.marked:
                cnt[op.eng] += 1
                op.token = (esem[op.eng], cnt[op.eng])
        for op in ops:
            if op.dsem is not None and op.dsem.group:
                op.token = (op.dsem.sem, op.dsem.total)
        per = {e: [] for e in self.ENGS}
        for op in ops:
            per[op.eng].append(op)
        stage_log = self.stage_log = []
        final_tokens = {}
        for op in ops:
            if op.token is not None:
                final_tokens[id(op.token[0])] = op.token

        def runner(engname):
            def run(e):
                seen = {}
                pos = 0
                for op in per[engname]:
                    waits = {}
                    for d in op.deps:
                        dop = ops[d]
                        if dop.token is None:
                            continue
                        if dop.dsem is None and op.dsem is None and dop.eng == "pe" and engname == "pe":
                            continue
                        s, v = dop.token
                        k = id(s)
                        if k not in waits or waits[k][1] < v:
                            waits[k] = (s, v)
                    for k, (s, v) in waits.items():
                        if seen.get(k, 0) >= v:
                            continue
                        seen[k] = v
                        e.wait_ge(s, v)
                        pos += 1
                    ins = op.fn(e)
                    if engname == "pe" and op.dsem is None:
                        pos += 1
                    stage_log.append((engname, pos, op.stage))
                    pos += 1
                    if op.dsem is not None:
                        ins.then_inc(op.dsem.sem, 16)
                    elif op.marked:
                        ins.then_inc(esem[engname], 1)
                if engname == "sp":
                    for k, (s, v) in final_tokens.items():
                        if seen.get(k, 0) >= v:
                            continue
                        e.wait_ge(s, v)
            return run

        block.tensor(runner("pe"))
        block.scalar(runner("act"))
        block.vector(runner("dve"))
        block.gpsimd(runner("pool"))
        block.sync(runner("sp"))
        return {e: len(per[e]) for e in self.ENGS}


class Paged:
    def __init__(self, t, nbytes, page=1024, name="ar"):
        self.t = t
        self.page = page
        self.pages = [Buf("%s%d" % (name, i)) for i in range((nbytes + page - 1) // page)]
        self.nbytes = nbytes

    def view(self, off, n, dt):
        sz = 4 if dt == F32 else 2
        assert off % sz == 0 and off + n * sz <= self.nbytes, (off, n, self.nbytes)
        ap = self.t[:, off // 2:(off + n * sz) // 2]
        if dt == F32:
            ap = ap.bitcast(F32)
        p0 = off // self.page
        p1 = (off + n * sz - 1) // self.page
        return V(ap, self.pages[p0:p1 + 1])


def build_program(nseq, seqlen, dbg_stop=None):
    ntile = seqlen // T
    ntok = nseq * seqlen
    nc = bass.Bass("TRN2", target_bir_lowering=False)
    P = Prog()

    def din(name, shape, dt=F32):
        return nc.dram_tensor(name, list(shape), dt, kind="ExternalInput").ap()

    x_d = din("x", [ntok, D])
    a_w_in_d = din("a_w_in", [D, 4096])
    a_w_out_d = din("a_w_out", [2048, D])
    b_w_in_d = din("b_w_in", [D, 6144])
    b_w_out_d = din("b_w_out", [2048, D])
    w1_d = din("mlp_w1", [2, D, 4096])
    w2_d = din("mlp_w2", [2, 4096, D])
    cols_d = din("cols", [128, 88])
    rows_d = din("rows", [3, 2048])
    wsT_d = din("wsT", [128, 8 * 128])
    tabs_d = din("tabs", [ntile, 8, 128, 2 * T])
    out_d = nc.dram_tensor("out", [ntok, D], F32, kind="ExternalOutput").ap()

    def dscr(name, nblk):
        return nc.dram_tensor(name, [nblk, 128, 4096], BF16, kind="Internal").ap()

    s_awin = dscr("s_awin", 8)
    s_awout = dscr("s_awout", 4)
    s_bwin = dscr("s_bwin", 12)
    s_bwout = dscr("s_bwout", 4)
    s_w1 = [dscr("s_w1_%d" % l, 8) for l in range(2)]
    s_w2 = [dscr("s_w2_%d" % l, 8) for l in range(2)]

    with ExitStack() as st:
        E = st.enter_context
        xres_t = E(nc.sbuf_tensor("xres", [128, 8 * T], F32))
        arena_t = E(nc.sbuf_tensor("arena", [128, ARENA_KIB * 512], BF16))
        st32_t = E(nc.sbuf_tensor("st32", [128, 8 * 512], F32))
        stbf_t = E(nc.sbuf_tensor("stbf", [128, 2 * 8 * 512], BF16))
        wslot_t = [E(nc.sbuf_tensor("wslot%d" % i, [128, 4096], BF16)) for i in range(NSLOT)]
        tab_t = [E(nc.sbuf_tensor("tab%d" % i, [128, 2 * T], F32)) for i in range(NTAB)]
        rstd_t = E(nc.sbuf_tensor("rstd_sb", [128, T], F32))
        cols_t = E(nc.sbuf_tensor("cols_sb", [128, 88], F32))
        epsc_t = E(nc.sbuf_tensor("epsc", [128, 1], F32))
        rowA_t = E(nc.sbuf_tensor("rowA", [1, 2048], BF16))
        bsb_t = E(nc.sbuf_tensor("bsb", [128, 1024], F32))
        identf_t = E(nc.sbuf_tensor("identf", [128, 128], F32))
        identb_t = E(nc.sbuf_tensor("identb", [128, 128], BF16))
        ones_t = E(nc.sbuf_tensor("ones_sb", [128, 128], BF16))
        mask4_t = E(nc.sbuf_tensor("mask4", [128, 512], BF16))
        wsTf_t = E(nc.sbuf_tensor("wsTf", [128, 1024], F32))
        vss_t = E(nc.sbuf_tensor("vss", [128, 16], F32))
        vrs_t = E(nc.sbuf_tensor("vrs", [128, 4], F32))
        fss_t = E(nc.sbuf_tensor("fss", [128, 8], F32))
        frs_t = E(nc.sbuf_tensor("frs", [128, 4], F32))
        banks = [E(nc.psum_tensor("bank%d" % i, [128, 512], F32)) for i in range(8)]
        esem = {e: E(nc.semaphore("s_" + e)) for e in Prog.ENGS}
        ds_w = [DSem(E(nc.semaphore("d_w%d" % i))) for i in range(NSLOT)]
        ds_tab = [DSem(E(nc.semaphore("d_t%d" % i))) for i in range(NTAB)]
        ds_x = [DSem(E(nc.semaphore("d_x%d" % i))) for i in range(NCH)]
        ds_o = [DSem(E(nc.semaphore("d_o%d" % i))) for i in range(NCH)]
        ds_c = DSem(E(nc.semaphore("d_c")), group=True)
        ds_g = DSem(E(nc.semaphore("d_g")))
        cast_blocks = [("awin", 8), ("awout", 4), ("w1_0", 8), ("w2_0", 8), ("bwin", 12), ("bwout", 4), ("w1_1", 8), ("w2_1", 8)]
        ds_cast = {(nm, b): DSem(E(nc.semaphore("d_cast_%s_%d" % (nm, b))), group=True) for nm, nb_ in cast_blocks for b in range(nb_)}
        block = E(nc.Block())

        AR = Paged(arena_t, ARENA_KIB * 1024)
        K = 1024

        xres = [V(xres_t[:, k * T:(k + 1) * T], [Buf("xres%d" % k)]) for k in range(8)]
        wslot = [V(wslot_t[i][:], [Buf("wslot%d" % i)]) for i in range(NSLOT)]
        tabv = [V(tab_t[i][:], [Buf("tab%d" % i)]) for i in range(NTAB)]
        rstd = V(rstd_t[:], [Buf("rstd")])
        b_cols = Buf("cols")
        b_const = Buf("const")
        b_rows = Buf("rows")
        b_wsT = Buf("wsT")
        b_vss = Buf("vss")
        b_vrs = Buf("vrs")
        b_fss = Buf("fss")
        b_frs = [Buf("frs%d" % c) for c in range(NCH)]
        st32 = [[V(st32_t[:, (h * 2 + kc) * 512:(h * 2 + kc + 1) * 512], [Buf("st32_%d_%d" % (h, kc))]) for kc in range(2)] for h in range(4)]
        stbf2 = [[[V(stbf_t[:, (par * 8 + h * 2 + kc) * 512:(par * 8 + h * 2 + kc + 1) * 512], [Buf("stbf%d_%d_%d" % (par, h, kc))]) for kc in range(2)] for h in range(4)] for par in range(2)]
        pq = [[Buf("ps%d" % b)] for b in range(8)]

        def PS(b, q0=0, nq=4):
            return V(banks[b][:, q0 * 128:(q0 + nq) * 128], pq[b])

        def PSB(b):
            return banks[b][:].bitcast(BF16)

        bank_ctr = [0]

        def nb():
            b = bank_ctr[0] % 8
            bank_ctr[0] += 1
            return b

        COL_NMIX, COL_NFFN, COL_FIN, COL_BU, COL_GV, COL_HG = 0, 16, 32, 40, 56, 72

        def col(c):
            return cols_t[:, c:c + 1]

        def mm(out, lhsT_ap, rhs_ap, rd, start, stop):
            P.add("pe", lambda e: e.matmul(out.ap, lhsT=lhsT_ap, rhs=rhs_ap, start=start, stop=stop), reads=rd, writes=out.bufs)

        def tr(out_ap, out_bufs, in_ap, rd, ident_ap):
            P.add("pe", lambda e: e.transpose(out=out_ap, in_=in_ap, identity=ident_ap), reads=rd + [b_const], writes=out_bufs)

        def act(out, in_, func, rd=(), bias=None, scale=None, accum=None, extra_w=()):
            kw = {}
            if bias is not None:
                kw["bias"] = bias
            if scale is not None:
                kw["scale"] = scale
            if accum is not None:
                kw["accum_out"] = accum
            P.add("act", lambda e: e.activation(out=out.ap, in_=in_.ap, func=func, **kw), reads=list(in_.bufs) + list(rd), writes=list(out.bufs) + list(extra_w))

        def tt(eng, out, a, b, op):
            P.add(eng, lambda e: e.tensor_tensor(out=out.ap, in0=a.ap, in1=b.ap, op=op), reads=list(a.bufs) + list(b.bufs), writes=out.bufs)

        def stt(eng, out, a, scalar, b, op0, op1, rd=()):
            P.add(eng, lambda e: e.scalar_tensor_tensor(out=out.ap, in0=a.ap, scalar=scalar, in1=b.ap, op0=op0, op1=op1),
                  reads=list(a.bufs) + list(b.bufs) + list(rd), writes=out.bufs)

        def dma(q, out_ap, in_ap, rd, wr, dsem):
            P.add(q, lambda e: e.dma_start(out=out_ap, in_=in_ap), reads=rd, writes=wr, dsem=dsem)

        P.add("pool", lambda e: e.memset(identf_t[:], 1.0), writes=[b_const])
        P.add("pool", lambda e: e.affine_select(out=identf_t[:], in_=identf_t[:], pattern=[[-1, 128]], compare_op=ALU.is_equal,
                                                fill=0.0, base=0, channel_multiplier=1), reads=[b_const], writes=[b_const])
        P.add("pool", lambda e: e.tensor_copy(out=identb_t[:], in_=identf_t[:]), reads=[b_const], writes=[b_const])
        P.add("pool", lambda e: e.memset(ones_t[:], 1.0), writes=[b_const])
        P.add("pool", lambda e: e.memset(epsc_t[:], EPS), writes=[b_const])
        P.add("pool", lambda e: e.memset(mask4_t[:], 1.0), writes=[b_const])
        P.add("pool", lambda e: e.affine_select(out=mask4_t[:].rearrange("p (h i) -> p h i", h=4), in_=mask4_t[:].rearrange("p (h i) -> p h i", h=4),
                                                pattern=[[0, 4], [1, 128]], compare_op=ALU.is_ge, fill=0.0, base=0, channel_multiplier=-1),
              reads=[b_const], writes=[b_const])
        rowsf_ap = arena_t[0:1, 0:12288].bitcast(F32)
        rowsf_b = AR.pages[0:24]
        dma("sp", cols_t[:], cols_d, [], [b_cols], ds_c)
        dma("sp", rowsf_ap, rows_d.rearrange("(o r) n -> o (r n)", o=1), [], rowsf_b, ds_c)
        dma("sp", wsTf_t[:], wsT_d, [], [b_wsT], ds_c)
        b_bsb = Buf("bsb")
        dma("sp", bsb_t[:], rows_d[2:3, 0:1024].partition_broadcast(128), [], [b_bsb], ds_c)
        P.add("dve", lambda e: e.tensor_copy(out=rowA_t[:], in_=rowsf_ap[:, 0:2048]), reads=rowsf_b, writes=[b_rows])

        P.add("pool", lambda e: e.affine_select(out=wsTf_t[:].rearrange("p (g t) -> p g t", g=8), in_=wsTf_t[:].rearrange("p (g t) -> p g t", g=8),
                                                pattern=[[0, 8], [1, 128]], compare_op=ALU.is_ge, fill=0.0, base=0, channel_multiplier=-1),
              reads=[b_wsT], writes=[b_wsT])

        b_scr = {}
        cast_q = []

        def pump_casts(n):
            for _ in range(n):
                if cast_q:
                    o_, i_, w_, d_ = cast_q.pop(0)
                    dma("pool", o_, i_, [], w_, d_)

        def cast_type1(w_ap, scr, ncols, name, b0=0):
            src = w_ap.rearrange("(kc p) (b c) -> b p kc c", p=128, c=512)
            for b in range(ncols // 512):
                bb = Buf("%s_%d" % (name, b0 + b))
                b_scr[(name, b0 + b)] = [bb]
                cast_q.append((scr[b0 + b].rearrange("p (kc c) -> p kc c", c=512), src[b], [bb], ds_cast[(name, b0 + b)]))

        def cast_type2(w_ap, scr, kdim, cb, name):
            nk = kdim // 128
            src = w_ap.rearrange("(kc p) (b c) -> b p kc c", p=128, c=cb)
            for b in range(1024 // cb):
                bb = [Buf("%s_%d_0" % (name, b)), Buf("%s_%d_1" % (name, b))]
                b_scr[(name, b)] = bb
                dst = scr[b].rearrange("p (kc c) -> p kc c", c=cb)
                h = nk // 2
                cast_q.append((dst[:, 0:h, :], src[b][:, 0:h, :], [bb[0]], ds_cast[(name, b)]))
                cast_q.append((dst[:, h:nk, :], src[b][:, h:nk, :], [bb[1]], ds_cast[(name, b)]))

        cast_type1(a_w_in_d, s_awin, 4096, "awin")
        cast_type2(a_w_out_d, s_awout, 2048, 256, "awout")
        cast_type1(w1_d[0], s_w1[0], 4096, "w1_0")
        cast_type2(w2_d[0], s_w2[0], 4096, 128, "w2_0")
        cast_type1(b_w_in_d, s_bwin, 6144, "bwin")
        cast_type1(b_w_out_d[0:1024, :], s_bwout, 1024, "bwout", 0)
        cast_type1(b_w_out_d[1024:2048, :], s_bwout, 1024, "bwout", 2)
        cast_type1(w1_d[1], s_w1[1], 4096, "w1_1")
        cast_type2(w2_d[1], s_w2[1], 4096, 128, "w2_1")
        pump_casts(16)

        wctr = [0]

        def load_w(scr, name, b):
            i = wctr[0] % NSLOT
            wctr[0] += 1
            dma("sp", wslot[i].ap, scr[b], b_scr[(name, b)], wslot[i].bufs, ds_w[i])
            return wslot[i], wslot_t[i]

        xn_off = 0

        def xn(kc, c0=0, n=T):
            return AR.view(xn_off + (kc * T + c0) * 2, n, BF16)

        def rmsnorm_to_xn(colbase):
            P.stage = P.stage.split(":")[0] + ":norm%d" % colbase
            sqv = [AR.view(8 * K + kc * T * 2, T, BF16) for kc in range(8)]
            xg = [AR.view(64 * K + i * 2048, T, F32) for i in range(4)]
            for kc in range(8):
                act(sqv[kc], xres[kc], AF.Square)
                if kc % 2 == 1:
                    act(xg[kc // 2], xres[kc], AF.Copy, rd=[b_cols], scale=col(colbase + kc))
            b = nb()
            for kc in range(8):
                mm(PS(b), ones_t[:], sqv[kc].ap, sqv[kc].bufs + [b_const], kc == 0, kc == 7)
            act(rstd, PS(b), AF.Sqrt, rd=[b_const], bias=epsc_t[:, 0:1], scale=1.0 / D)
            P.add("dve", lambda e: e.reciprocal(out=rstd.ap, in_=rstd.ap), reads=rstd.bufs, writes=rstd.bufs)
            for kc in range(8):
                if kc % 2 == 0:
                    stt("dve", xn(kc), xres[kc], col(colbase + kc), rstd, ALU.mult, ALU.mult, rd=[b_cols])
                else:
                    tt("pool", xn(kc), xg[kc // 2], rstd, ALU.mult)

        def load_x_tile(tok0):
            xin = [AR.view(16 * K + c * 4096, 1024, F32) for c in range(NCH)]
            for c in range(NCH):
                dma("sp", xin[c].ap, x_d[tok0 + c * 128: tok0 + (c + 1) * 128, :], [], xin[c].bufs, ds_x[c])
            for kc in range(8):
                b = nb()
                for c in range(NCH):
                    o = PS(b, c, 1)
                    tr(o.ap, o.bufs, xin[c].ap[:, kc * 128:(kc + 1) * 128], xin[c].bufs, identf_t[:])
                act(xres[kc], PS(b), AF.Copy)

        def mixer_a():
            rmsnorm_to_xn(COL_NMIX + 0)
            pump_casts(36)
            u = [AR.view(16 * K + fc * T * 2, T, BF16) for fc in range(16)]
            vbf = [AR.view(32 * K + c * 4096, 2048, BF16) for c in range(NCH)]
            wTs = [AR.view(48 * K + i * 2048, 1024, BF16) for i in range(4)]
            junk = AR.view(56 * K, 512, BF16)
            sgt = [AR.view(58 * K + i * 1024, 128, F32) for i in range(4)]
            P.stage = P.stage.split(":")[0] + ":A_v"
            b_vssc = [Buf("vss%d" % c) for c in range(NCH)]
            b_vrsc = [Buf("vrs%d" % c) for c in range(NCH)]
            P.add("pool", lambda e: e.memset(vss_t[:], 0.0), writes=b_vssc)
            for vb in range(4):
                ws, wt = load_w(s_awin, "awin", 4 + vb)
                for c in range(NCH):
                    b = nb()
                    mm(PS(b), ones_t[0:1, :], rowA_t[0:1, vb * 512:(vb + 1) * 512], [b_const, b_rows], True, False)
                    for kc in range(8):
                        xv = xn(kc, c * 128, 128)
                        mm(PS(b), xv.ap, wt[:, kc * 512:(kc + 1) * 512], ws.bufs + xv.bufs, False, kc == 7)
                    vv = V(vbf[c].ap[:, vb * 512:(vb + 1) * 512], vbf[c].bufs)
                    act(vv, PS(b), AF.Gelu_apprx_tanh)
                    act(junk, vv, AF.Square, accum=vss_t[:, c * 4 + vb: c * 4 + vb + 1], extra_w=[b_vssc[c]])
            for c in range(NCH):
                P.add("dve", (lambda c: lambda e: e.tensor_reduce(out=vrs_t[:, c:c + 1], in_=vss_t[:, c * 4:(c + 1) * 4], axis=mybir.AxisListType.X, op=ALU.add))(c),
                      reads=[b_vssc[c]], writes=[b_vrsc[c]])
                P.add("act", (lambda c: lambda e: e.activation(out=vrs_t[:, c:c + 1], in_=vrs_t[:, c:c + 1], func=AF.Sqrt, bias=epsc_t[:, 0:1], scale=1.0 / 2048))(c),
                      reads=[b_vrsc[c], b_const], writes=[b_vrsc[c]])
                P.add("dve", (lambda c: lambda e: e.reciprocal(out=vrs_t[:, c:c + 1], in_=vrs_t[:, c:c + 1]))(c), reads=[b_vrsc[c]], writes=[b_vrsc[c]])
                P.add("act", (lambda w_c, c: lambda e: e.activation(out=w_c.ap, in_=wsTf_t[:], func=AF.Copy, scale=vrs_t[:, c:c + 1]))(wTs[c], c),
                      reads=[b_wsT, b_vrsc[c]], writes=wTs[c].bufs)

            def sgate_group(gq):
                for c in range(NCH):
                    w_c = wTs[c]
                    b = nb()
                    for r in range(4):
                        fc = gq * 4 + r
                        g = fc // 2
                        o = PS(b, r, 1)
                        mm(o, vbf[c].ap[:, fc * 128:(fc + 1) * 128], w_c.ap[:, g * 128:(g + 1) * 128], vbf[c].bufs + w_c.bufs, True, True)
                    for r in range(4):
                        fc = gq * 4 + r
                        g = fc // 2
                        o = PS(b, r, 1)
                        uu = V(u[fc].ap[:, c * 128:(c + 1) * 128], u[fc].bufs)
                        st_ = sgt[r]
                        stt("dve", st_, o, col(COL_GV + fc), V(bsb_t[:, g * 128:(g + 1) * 128], [b_bsb]), ALU.mult, ALU.add, rd=[b_cols])
                        tt("pool", uu, st_, uu, ALU.mult)

            for blk in range(4):
                P.stage = P.stage.split(":")[0] + ":A_u"
                ws, wt = load_w(s_awin, "awin", blk)
                for m in range(4):
                    b = nb()
                    for kc in range(8):
                        mm(PS(b), wt[:, kc * 512 + m * 128: kc * 512 + (m + 1) * 128], xn(kc).ap, ws.bufs + xn(kc).bufs, kc == 0, kc == 7)
                    fc = blk * 4 + m
                    act(u[fc], PS(b), AF.Gelu_apprx_tanh, rd=[b_cols], bias=col(COL_BU + fc))
                if blk >= 1:
                    P.stage = P.stage.split(":")[0] + ":A_sg"
                    sgate_group(blk - 1)
            P.stage = P.stage.split(":")[0] + ":A_sg"
            sgate_group(3)
            P.stage = P.stage.split(":")[0] + ":A_out"
            w01 = [load_w(s_awout, "awout", 0), load_w(s_awout, "awout", 1)]
            b01 = [nb() for _ in range(4)]
            for lo, hi in ((0, 12), (12, 16)):
                for m in range(4):
                    ws, wt = w01[m // 2]
                    mmi = m % 2
                    for kc in range(lo, hi):
                        mm(PS(b01[m]), wt[:, kc * 256 + mmi * 128: kc * 256 + (mmi + 1) * 128], u[kc].ap, ws.bufs + u[kc].bufs, kc == 0, kc == 15)
                    if hi == 16:
                        tt("dve", xres[m], xres[m], PS(b01[m]), ALU.add)
            for blk in range(2, 4):
                ws, wt = load_w(s_awout, "awout", blk)
                for mmi in range(2):
                    m = blk * 2 + mmi
                    b = nb()
                    for kc in range(16):
                        mm(PS(b), wt[:, kc * 256 + mmi * 128: kc * 256 + (mmi + 1) * 128], u[kc].ap, ws.bufs + u[kc].bufs, kc == 0, kc == 15)
                    tt("dve", xres[m], xres[m], PS(b), ALU.add)

        def mlp(l):
            colbase = COL_NFFN + 8 * l
            P.stage = P.stage.split(":")[0] + ":norm%d" % colbase
            sqv = [AR.view(8 * K + kc * T * 2, T, BF16) for kc in range(8)]
            for kc in range(8):
                act(xn(kc), xres[kc], AF.Copy, rd=[b_cols], scale=col(colbase + kc))
            for kc in range(8):
                act(sqv[kc], xres[kc], AF.Square)
            b = nb()
            for kc in range(8):
                mm(PS(b), ones_t[:], sqv[kc].ap, sqv[kc].bufs + [b_const], kc == 0, kc == 7)
            act(rstd, PS(b), AF.Sqrt, rd=[b_const], bias=epsc_t[:, 0:1], scale=1.0 / D)
            P.add("dve", lambda e: e.reciprocal(out=rstd.ap, in_=rstd.ap), reads=rstd.bufs, writes=rstd.bufs)
            H = [AR.view(32 * K + fc * T * 2, T, BF16) for fc in range(32)]
            rt = [AR.view(64 * K + i * 2048, T, F32) for i in range(4)]
            ri = 0
            P.stage = P.stage.split(":")[0] + ":m%d_w1" % l
            for blk in range(8):
                ws, wt = load_w(s_w1[l], "w1_%d" % l, blk)
                for m in range(4):
                    b = nb()
                    for kc in range(8):
                        mm(PS(b), wt[:, kc * 512 + m * 128: kc * 512 + (m + 1) * 128], xn(kc).ap, ws.bufs + xn(kc).bufs, kc == 0, kc == 7)
                    r = rt[ri % 4]
                    ri += 1
                    stt("dve", r, PS(b), 0.0, rstd, ALU.max, ALU.mult)
                    tt("pool", H[blk * 4 + m], r, r, ALU.mult)
                    if blk >= 1:
                        pump_casts(2)
            P.stage = P.stage.split(":")[0] + ":m%d_w2" % l
            for m in range(8):
                ws, wt = load_w(s_w2[l], "w2_%d" % l, m)
                b = nb()
                for kc in range(32):
                    mm(PS(b), wt[:, kc * 128:(kc + 1) * 128], H[kc].ap, ws.bufs + H[kc].bufs, kc == 0, kc == 31)
                tt("dve", xres[m], xres[m], PS(b), ALU.add)

        tabctr = [0]

        def mixer_b(tile_idx):
            rmsnorm_to_xn(COL_NMIX + 8)
            qT = [AR.view(16 * K + i * T * 2, T, BF16) for i in range(8)]
            kT = [AR.view(24 * K + i * T * 2, T, BF16) for i in range(8)]
            ktok = [AR.view(32 * K + c * 2048, 1024, BF16) for c in range(NCH)]
            vtok = [AR.view(40 * K + c * 4096, 2048, BF16) for c in range(NCH)]
            sg = [AR.view(56 * K + fc * T * 2, T, BF16) for fc in range(16)]
            rtmp = [AR.view(64 * K + i * 2048, T, F32) for i in range(4)]
            yoff = [8 * K + i * 1024 for i in range(8)] + [16 * K + i * 1024 for i in range(4)] + [24 * K + i * 1024 for i in range(4)]
            yT = [AR.view(yoff[fc], T, BF16) for fc in range(16)]
            PT = [AR.view(72 * K + i * 1024, 512, BF16) for i in range(2)]
            sqh = [AR.view(74 * K + i * 1024, 512, BF16) for i in range(2)]
            rs = [AR.view(76 * K + i * 2048, 512, F32) for i in range(2)]
            y1h = [AR.view(80 * K + i * 2048, 512, F32) for i in range(2)]
            sgtmp = [AR.view(84 * K + i * 2048, 512, F32) for i in range(2)]
            fbank = [0]

            def bank_free():
                return nb()

            def bank_fill():
                fbank[0] ^= 1
                return 6 + fbank[0]

            def rotary(isk, h, b1, b2):
                dst = kT if isk else qT
                ti = tabctr[0] % NTAB
                tabctr[0] += 1
                dma("pool", tabv[ti].ap, tabs_d[tile_idx, isk * 4 + h], [], tabv[ti].bufs, ds_tab[ti])
                Cd = V(tab_t[ti][:, 0:T], tabv[ti].bufs)
                Sd = V(tab_t[ti][:, T:2 * T], tabv[ti].bufs)
                t1 = PS(b1)
                t2 = PS(b2)
                tt("dve", rtmp[0], t1, Cd, ALU.mult)
                tt("dve", rtmp[1], t2, Sd, ALU.mult)
                tt("pool", dst[h * 2 + 0], rtmp[0], rtmp[1], ALU.subtract)
                tt("dve", rtmp[2], t2, Cd, ALU.mult)
                tt("dve", rtmp[3], t1, Sd, ALU.mult)
                tt("pool", dst[h * 2 + 1], rtmp[2], rtmp[3], ALU.add)

            def units_qk(blk, balloc):
                st_ = {}

                def unit(m):
                    def f():
                        if m == 0:
                            st_["w"] = load_w(s_bwin, "bwin", blk)
                            st_["b"] = []
                        ws, wt = st_["w"]
                        b = balloc()
                        st_["b"].append(b)
                        for kc in range(8):
                            mm(PS(b), wt[:, kc * 512 + m * 128: kc * 512 + (m + 1) * 128], xn(kc).ap, ws.bufs + xn(kc).bufs, kc == 0, kc == 7)
                        if m % 2 == 1:
                            rotary(blk // 2, (blk % 2) * 2 + m // 2, st_["b"][m - 1], st_["b"][m])
                    return f
                return [unit(m) for m in range(4)]

            def units_v(vb, balloc):
                st_ = {}

                def unit(c):
                    def f():
                        if c == 0:
                            st_["w"] = load_w(s_bwin, "bwin", 4 + vb)
                        ws, wt = st_["w"]
                        b = balloc()
                        for kc in range(8):
                            xv = xn(kc, c * 128, 128)
                            mm(PS(b), xv.ap, wt[:, kc * 512:(kc + 1) * 512], ws.bufs + xv.bufs, kc == 0, kc == 7)
                        act(V(vtok[c].ap[:, vb * 512:(vb + 1) * 512], vtok[c].bufs), PS(b), AF.Copy)
                    return f
                return [unit(c) for c in range(NCH)]

            def units_g(gb, balloc):
                st_ = {}

                def unit(m):
                    def f():
                        if m == 0:
                            st_["w"] = load_w(s_bwin, "bwin", 8 + gb)
                        ws, wt = st_["w"]
                        b = balloc()
                        for kc in range(8):
                            mm(PS(b), wt[:, kc * 512 + m * 128: kc * 512 + (m + 1) * 128], xn(kc).ap, ws.bufs + xn(kc).bufs, kc == 0, kc == 7)
                        fc = gb * 4 + m
                        tmp_ = sgtmp[fc % 2]
                        act(tmp_, PS(b), AF.Silu)
                        act(sg[fc], tmp_, AF.Copy, rd=[b_cols], scale=col(COL_HG + fc))
                    return f
                return [unit(m) for m in range(4)]

            def transposes(pair, balloc):
                for c in range(NCH):
                    b = balloc()
                    pb = PSB(b)
                    for j in range(4):
                        i = pair * 4 + j
                        tr(pb[:, j * 128:(j + 1) * 128], pq[b], kT[i].ap[:, c * 128:(c + 1) * 128], kT[i].bufs, identb_t[:])
                    P.add("act", (lambda kt, pb, pair: lambda e: e.activation(out=kt.ap[:, pair * 512:(pair + 1) * 512], in_=pb[:, 0:512], func=AF.Copy))(ktok[c], pb, pair),
                          reads=pq[b], writes=ktok[c].bufs)

            def st_S(pair, c):
                cs = slice(c * 128, (c + 1) * 128)
                for s_ in range(2):
                    h = pair * 2 + s_
                    for half in range(2):
                        i = h * 2 + half
                        mm(PS(0, s_, 1), kT[i].ap[:, cs], qT[i].ap[:, cs], kT[i].bufs + qT[i].bufs, half == 0, half == 1)
                pt = PT[c % 2]
                P.add("dve", (lambda pt: lambda e: e.tensor_tensor(out=pt.ap[:, 0:256], in0=banks[0][:, 0:256], in1=mask4_t[:, 0:256], op=ALU.mult))(pt),
                      reads=pq[0] + [b_const], writes=pt.bufs)

            def st_O(pair, c, s_):
                h = pair * 2 + s_
                cs = slice(c * 128, (c + 1) * 128)
                pt = PT[c % 2]
                b = 4 + s_
                for vc in range(4):
                    o = PS(b, vc, 1)
                    mm(o, vtok[c].ap[:, h * 512 + vc * 128: h * 512 + (vc + 1) * 128], pt.ap[:, s_ * 128:(s_ + 1) * 128], vtok[c].bufs + pt.bufs, True, False)
                    for kc in range(2):
                        sb_ = stbf2[c % 2][h][kc]
                        mm(o, sb_.ap[:, vc * 128:(vc + 1) * 128], qT[h * 2 + kc].ap[:, cs], sb_.bufs + qT[h * 2 + kc].bufs, False, kc == 1)
                act(sqh[s_], PS(b), AF.Square)

            def st_N(pair, c, s_):
                sq_ = sqh[s_]
                for vc in range(4):
                    mm(PS(1, s_, 1), ones_t[:], sq_.ap[:, vc * 128:(vc + 1) * 128], sq_.bufs + [b_const], vc == 0, vc == 3)
                if s_ == 0:
                    return
                r2 = V(rs[c % 2].ap[:, 0:256], rs[c % 2].bufs)
                act(r2, PS(1, 0, 2), AF.Sqrt, rd=[b_const], bias=epsc_t[:, 0:1], scale=1.0 / 512)
                P.add("dve", (lambda r2: lambda e: e.reciprocal(out=r2.ap, in_=r2.ap))(r2), reads=r2.bufs, writes=r2.bufs)
                for ss in range(2):
                    h = pair * 2 + ss
                    b = 4 + ss
                    yh = y1h[ss]
                    y4 = yh.ap.rearrange("p (v i) -> p v i", v=4)
                    o4 = banks[b][:].rearrange("p (v i) -> p v i", v=4)
                    rb = r2.ap[:, ss * 128:(ss + 1) * 128].unsqueeze(1).to_broadcast([128, 4, 128])
                    P.add("dve", (lambda y4, o4, rb: lambda e: e.tensor_tensor(out=y4, in0=o4, in1=rb, op=ALU.mult))(y4, o4, rb), reads=pq[b] + r2.bufs, writes=yh.bufs)
                    fc0 = h * 4
                    sg4 = arena_t[:, (56 * K + fc0 * 1024) // 2:(56 * K + (fc0 + 4) * 1024) // 2].rearrange("p (v t) -> p v t", v=4)[:, :, c * 128:(c + 1) * 128]
                    yo4 = arena_t[:, yoff[fc0] // 2:(yoff[fc0] + 4 * 1024) // 2].rearrange("p (v t) -> p v t", v=4)[:, :, c * 128:(c + 1) * 128]
                    sgb = [bb for fc in range(fc0, fc0 + 4) for bb in sg[fc].bufs]
                    yob = [bb for fc in range(fc0, fc0 + 4) for bb in yT[fc].bufs]
                    P.add("pool", (lambda yo4, y4, sg4: lambda e: e.tensor_tensor(out=yo4, in0=y4, in1=sg4, op=ALU.mult))(yo4, y4, sg4), reads=yh.bufs + sgb, writes=yob)

            def st_U(pair, c, s_):
                h = pair * 2 + s_
                gC = float(np.float32(1.0 - 2.0 ** (-5.0 - h)) ** 128)
                for kc in range(2):
                    bd = 2 + kc
                    mm(PS(bd), ktok[c].ap[:, h * 256 + kc * 128: h * 256 + (kc + 1) * 128], vtok[c].ap[:, h * 512:(h + 1) * 512], ktok[c].bufs + vtok[c].bufs, True, True)
                    s32 = st32[h][kc]
                    P.add("dve", (lambda s32, bd, gC: lambda e: e.scalar_tensor_tensor(out=s32.ap, in0=s32.ap, scalar=gC, in1=banks[bd][:], op0=ALU.mult, op1=ALU.add))(s32, bd, gC),
                          reads=s32.bufs + pq[bd], writes=s32.bufs)
                    act(stbf2[(c + 1) % 2][h][kc], s32, AF.Copy, scale=gC)

            def retention(pair, fillers, front=0):
                steps = [lambda: st_S(pair, 0)]
                for c in range(NCH):
                    steps.append((lambda c: lambda: st_U(pair, c, 0))(c))
                    steps.append((lambda c: lambda: st_O(pair, c, 0))(c))
                    steps.append((lambda c: lambda: st_U(pair, c, 1))(c))
                    steps.append((lambda c: lambda: st_O(pair, c, 1))(c))
                    if c + 1 < NCH:
                        steps.append((lambda c: lambda: st_S(pair, c + 1))(c))
                    steps.append((lambda c: lambda: st_N(pair, c, 0))(c))
                    steps.append((lambda c: lambda: st_N(pair, c, 1))(c))
                nf = len(fillers)
                done = 0
                for i, stp in enumerate(steps):
                    stp()
                    if front:
                        want = min(front, 2 * (i + 1)) if 2 * (i + 1) <= front + 1 else front + ((i + 1 - front // 2) * (nf - front) + len(steps) - front // 2 - 1) // (len(steps) - front // 2)
                    else:
                        want = ((i + 1) * nf + len(steps) - 1) // len(steps)
                    while done < min(want, nf):
                        fillers[done]()
                        done += 1
                while done < nf:
                    fillers[done]()
                    done += 1

            def units_wo(half, balloc):
                us = []
                for j in range(2):
                    st_ = {}

                    def unit(j, mmi, st_=st_):
                        def f():
                            if mmi == 0:
                                st_["w"] = load_w(s_bwout, "bwout", half * 2 + j)
                            ws, wt = st_["w"]
                            m = j * 4 + mmi
                            b = balloc()
                            for kc in range(8):
                                yk = yT[half * 8 + kc]
                                mm(PS(b), wt[:, kc * 512 + mmi * 128: kc * 512 + (mmi + 1) * 128], yk.ap, ws.bufs + yk.bufs, kc == 0, kc == 7)
                            tt("dve", xres[m], xres[m], PS(b), ALU.add)
                        return f
                    us += [unit(j, mmi) for mmi in range(4)]
                return us

            P.stage = P.stage.split(":")[0] + ":B_pre"
            for u_ in units_qk(0, bank_free) + units_qk(2, bank_free) + units_v(0, bank_free) + units_v(1, bank_free) + units_g(0, bank_free) + units_g(1, bank_free):
                u_()
            P.stage = P.stage.split(":")[0] + ":B_retA"
            transposes(0, bank_free)
            retention(0, units_qk(1, bank_fill) + units_qk(3, bank_fill) + units_v(2, bank_fill) + units_v(3, bank_fill))
            P.stage = P.stage.split(":")[0] + ":B_retB"
            transposes(1, bank_fill)
            retention(1, units_g(2, bank_fill) + units_g(3, bank_fill) + units_wo(0, bank_fill), front=8)
            P.stage = P.stage.split(":")[0] + ":B_out2"
            for u_ in units_wo(1, bank_free):
                u_()

        def store_tile(tok0, do_norm):
            P.stage = P.stage.split(":")[0] + ":fin"
            ost = [AR.view(48 * K + c * 4096, 1024, F32) for c in range(NCH)]
            if do_norm:
                gbc = AR.view(32 * K, 1024, F32)
                fjunk = AR.view(36 * K, 512, BF16)
                dma("pool", gbc.ap, rows_d[2:3, 1024:2048].partition_broadcast(128), [], gbc.bufs, ds_g)
                P.add("pool", lambda e: e.memset(fss_t[:], 0.0), writes=[b_fss])
            for c in range(NCH):
                bh = [nb(), nb()]
                for half in range(2):
                    for j in range(4):
                        kc = half * 4 + j
                        o = PS(bh[half], j, 1)
                        tr(o.ap, o.bufs, xres[kc].ap[:, c * 128:(c + 1) * 128], xres[kc].bufs, identf_t[:])
                if do_norm:
                    for half in range(2):
                        act(fjunk, PS(bh[half]), AF.Square, accum=fss_t[:, c * 2 + half: c * 2 + half + 1], extra_w=[b_fss])
                    P.add("dve", (lambda c: lambda e: e.tensor_reduce(out=frs_t[:, c:c + 1], in_=fss_t[:, c * 2:c * 2 + 2], axis=mybir.AxisListType.X, op=ALU.add))(c),
                          reads=[b_fss], writes=[b_frs[c]])
                    P.add("act", (lambda c: lambda e: e.activation(out=frs_t[:, c:c + 1], in_=frs_t[:, c:c + 1], func=AF.Sqrt, bias=epsc_t[:, 0:1], scale=1.0 / D))(c),
                          reads=[b_frs[c], b_const], writes=[b_frs[c]])
                    P.add("dve", (lambda c: lambda e: e.reciprocal(out=frs_t[:, c:c + 1], in_=frs_t[:, c:c + 1]))(c), reads=[b_frs[c]], writes=[b_frs[c]])
                    for half in range(2):
                        stt("dve", V(ost[c].ap[:, half * 512:(half + 1) * 512], ost[c].bufs), PS(bh[half]), frs_t[:, c:c + 1],
                            V(gbc.ap[:, half * 512:(half + 1) * 512], gbc.bufs), ALU.mult, ALU.mult, rd=[b_frs[c]])
                else:
                    for half in range(2):
                        act(V(ost[c].ap[:, half * 512:(half + 1) * 512], ost[c].bufs), PS(bh[half]), AF.Copy)
                dma("act", out_d[tok0 + c * 128: tok0 + (c + 1) * 128, :], ost[c].ap, ost[c].bufs, [], ds_o[c])

        for s in range(nseq):
            for h in range(4):
                for kc in range(2):
                    P.add("pool", (lambda v: lambda e: e.memset(v.ap, 0.0))(st32[h][kc]), writes=st32[h][kc].bufs)
                    P.add("pool", (lambda v: lambda e: e.memset(v.ap, 0.0))(stbf2[0][h][kc]), writes=stbf2[0][h][kc].bufs)
            for ti in range(ntile):
                tok0 = s * seqlen + ti * T
                P.stage = "t%d:ldx" % (s * ntile + ti)
                load_x_tile(tok0)
                if dbg_stop == "load":
                    store_tile(tok0, False)
                    continue
                mixer_a()
                if dbg_stop == "mixa":
                    store_tile(tok0, False)
                    continue
                mlp(0)
                if dbg_stop == "mlp0":
                    store_tile(tok0, False)
                    continue
                mixer_b(ti)
                if dbg_stop == "mixb":
                    store_tile(tok0, False)
                    continue
                mlp(1)
                if dbg_stop == "mlp1":
                    store_tile(tok0, False)
                    continue
                store_tile(tok0, True)

        counts = P.emit(nc, block, esem)
        import os
        if os.environ.get("MK_STAGE_LOG"):
            import json
            json.dump(P.stage_log, open(os.environ["MK_STAGE_LOG"], "w"))
    return nc, counts


def _tables(seqlen):
    f32 = np.float32
    pos = np.arange(seqlen, dtype=f32)
    inv_freq = (1.0 / (f32(10000.0) ** np.linspace(0.0, 1.0, 128, dtype=f32))).astype(f32)
    ang = (pos[:, None] * inv_freq[None, :]).astype(f32)
    cos = np.cos(ang).astype(f32).T
    sin = np.sin(ang).astype(f32).T
    idx = (np.arange(seqlen) % 128).astype(f32)
    ntile = seqlen // T
    tabs = np.zeros((ntile, 8, 128, 2 * T), dtype=f32)
    for h in range(4):
        lg = np.log(f32(1.0) - f32(2.0) ** f32(-5.0 - h)).astype(f32)
        qd = np.exp(lg * (idx + 1.0)).astype(f32)
        kd = (np.exp(-lg * (idx + 1.0)) * f32(256.0 ** -0.5)).astype(f32)
        for isk, dd in ((0, qd), (1, kd)):
            C = (cos * dd[None, :]).astype(f32).reshape(128, ntile, T)
            S_ = (sin * dd[None, :]).astype(f32).reshape(128, ntile, T)
            tabs[:, isk * 4 + h, :, 0:T] = C.transpose(1, 0, 2)
            tabs[:, isk * 4 + h, :, T:2 * T] = S_.transpose(1, 0, 2)
    return tabs


def _colize(v):
    v = np.asarray(v, dtype=np.float32)
    return v.reshape(-1, 128).T


def _prep_shared(inp, seqlen):
    cols = np.concatenate([
        _colize(inp["norm_mix_g"][0]), _colize(inp["norm_mix_g"][1]),
        _colize(inp["norm_ffn_g"][0]), _colize(inp["norm_ffn_g"][1]),
        _colize(inp["final_norm_g"]),
        _colize(inp["a_b_in"][0][:2048]),
        _colize(inp["a_v_norm_g"][0]),
        _colize(inp["b_head_norm_g"][0]),
    ], axis=1).astype(np.float32)
    assert cols.shape == (128, 88)
    rows = np.zeros((3, 2048), dtype=np.float32)
    rows[0] = inp["a_b_in"][0][2048:]
    rows[1] = inp["a_v_norm_g"][0]
    rows[2, :1024] = np.asarray(inp["a_b_s"][0]).reshape(-1)
    rows[2, 1024:] = np.asarray(inp["final_norm_g"]).reshape(-1)
    wsT = np.ascontiguousarray(np.asarray(inp["a_w_s"][0]).transpose(2, 0, 1)).reshape(128, 1024)
    shared = {
        "a_w_in": np.ascontiguousarray(inp["a_w_in"][0]),
        "a_w_out": np.ascontiguousarray(inp["a_w_out"][0]),
        "b_w_in": np.ascontiguousarray(inp["b_w_in"][0]),
        "b_w_out": np.ascontiguousarray(inp["b_w_out"][0]),
        "mlp_w1": np.ascontiguousarray(inp["mlp_w1"]),
        "mlp_w2": np.ascontiguousarray(inp["mlp_w2"]),
        "cols": np.ascontiguousarray(cols),
        "rows": rows,
        "wsT": wsT,
        "tabs": _tables(seqlen),
    }
    return shared


_CACHE = {}


def run(inp, ncores, nseq, seqlen, dbg_stop=None):
    inp = {k: np.asarray(v, dtype=np.float32) for k, v in inp.items()}
    key = (nseq, seqlen, dbg_stop)
    if key not in _CACHE:
        _CACHE[key] = build_program(nseq, seqlen, dbg_stop)
    nc, counts = _CACHE[key]
    shared = _prep_shared(inp, seqlen)
    x = inp["x"]
    in_maps = []
    for i in range(ncores):
        m = dict(shared)
        m["x"] = np.ascontiguousarray(x[i * nseq:(i + 1) * nseq].reshape(nseq * seqlen, D))
        in_maps.append(m)
    res = run_bass_kernel_spmd(nc, in_maps, core_ids=list(range(ncores)))
    outs = [np.asarray(r["out"]).reshape(nseq, seqlen, D) for r in res.results]
    return np.concatenate(outs, axis=0).astype(np.float32)


def kernel(**inputs):
    return run(inputs, 8, 2, 4096)
```

```python
import numpy as np
from contextlib import ExitStack
import concourse.bass as bass
import concourse.mybir as mybir
from concourse.bass_utils import run_bass_kernel_spmd

F32 = mybir.dt.float32
BF16 = mybir.dt.bfloat16
ALU = mybir.AluOpType
AF = mybir.ActivationFunctionType

D = 1024
T = 512
NCH = 4
NSLOT = 4
NTAB = 2
ARENA_KIB = 88
EPS = 1e-6


class Buf:
    __slots__ = ("name", "lw", "rd")

    def __init__(self, name):
        self.name = name
        self.lw = None
        self.rd = []


class DSem:
    def __init__(self, sem, group=False):
        self.sem = sem
        self.total = 0
        self.group = group


class Op:
    __slots__ = ("eng", "fn", "deps", "marked", "token", "dsem", "idx", "stage")


class V:
    __slots__ = ("ap", "bufs")

    def __init__(self, ap, bufs):
        self.ap = ap
        self.bufs = bufs


class Prog:
    ENGS = ["pe", "act", "dve", "pool", "sp"]

    def __init__(self):
        self.ops = []
        self.stage = "setup"

    def add(self, eng, fn, reads=(), writes=(), dsem=None):
        op = Op()
        op.eng = eng
        op.fn = fn
        op.dsem = dsem
        op.idx = len(self.ops)
        op.marked = False
        op.token = None
        op.stage = self.stage
        deps = set()
        for b in reads:
            if b.lw is not None:
                deps.add(b.lw)
        for b in writes:
            if b.lw is not None:
                deps.add(b.lw)
            deps.update(b.rd)
        for b in writes:
            b.lw = op.idx
            b.rd = []
        for b in reads:
            if b.lw != op.idx:
                b.rd.append(op.idx)
        deps.discard(op.idx)
        op.deps = deps
        self.ops.append(op)
        return op

    def emit(self, nc, block, esem):
        ops = self.ops
        for op in ops:
            for d in op.deps:
                dop = ops[d]
                if dop.dsem is None:
                    if dop.eng == "pe" and op.eng == "pe" and op.dsem is None:
                        continue
                    dop.marked = True
        cnt = {e: 0 for e in self.ENGS}
        for op in ops:
            if op.dsem is not None:
                op.dsem.total += 16
                op.token = (op.dsem.sem, op.dsem.total)
            elif op.marked:
                cnt[op.eng] += 1
                op.token = (esem[op.eng], cnt[op.eng])
        for op in ops:
            if op.dsem is not None and op.dsem.group:
                op.token = (op.dsem.sem, op.dsem.total)
        per = {e: [] for e in self.ENGS}
        for op in ops:
            per[op.eng].append(op)
        stage_log = self.stage_log = []
        final_tokens = {}
        for op in ops:
            if op.token is not None:
                final_tokens[id(op.token[0])] = op.token

        def runner(engname):
            def run(e):
                seen = {}
                pos = 0
                for op in per[engname]:
                    waits = {}
                    for d in op.deps:
                        dop = ops[d]
                        if dop.token is None:
                            continue
                        if dop.dsem is None and op.dsem is None and dop.eng == "pe" and engname == "pe":
                            continue
                        s, v = dop.token
                        k = id(s)
                        if k not in waits or waits[k][1] < v:
                            waits[k] = (s, v)
                    for k, (s, v) in waits.items():
                        if seen.get(k, 0) >= v:
                            continue
                        seen[k] = v
                        e.wait_ge(s, v)
                        pos += 1
                    ins = op.fn(e)
                    if engname == "pe" and op.dsem is None:
                        pos += 1
                    stage_log.append((engname, pos, op.stage))
                    pos += 1
                    if op.dsem is not None:
                        ins.then_inc(op.dsem.sem, 16)
                    elif op.marked:
                        ins.then_inc(esem[engname], 1)
                if engname == "sp":
                    for k, (s, v) in final_tokens.items():
                        if seen.get(k, 0) >= v:
                            continue
                        e.wait_ge(s, v)
            return run

        block.tensor(runner("pe"))
        block.scalar(runner("act"))
        block.vector(runner("dve"))
        block.gpsimd(runner("pool"))
        block.sync(runner("sp"))
        return {e: len(per[e]) for e in self.ENGS}


class Paged:
    def __init__(self, t, nbytes, page=1024, name="ar"):
        self.t = t
        self.page = page
        self.pages = [Buf("%s%d" % (name, i)) for i in range((nbytes + page - 1) // page)]
        self.nbytes = nbytes

    def view(self, off, n, dt):
        sz = 4 if dt == F32 else 2
        assert off % sz == 0 and off + n * sz <= self.nbytes, (off, n, self.nbytes)
        ap = self.t[:, off // 2:(off + n * sz) // 2]
        if dt == F32:
            ap = ap.bitcast(F32)
        p0 = off // self.page
        p1 = (off + n * sz - 1) // self.page
        return V(ap, self.pages[p0:p1 + 1])


def build_program(nseq, seqlen, dbg_stop=None):
    ntile = seqlen // T
    ntok = nseq * seqlen
    nc = bass.Bass("TRN2", target_bir_lowering=False)
    P = Prog()

    def din(name, shape, dt=F32):
        return nc.dram_tensor(name, list(shape), dt, kind="ExternalInput").ap()

    x_d = din("x", [ntok, D])
    a_w_in_d = din("a_w_in", [D, 4096])
    a_w_out_d = din("a_w_out", [2048, D])
    b_w_in_d = din("b_w_in", [D, 6144])
    b_w_out_d = din("b_w_out", [2048, D])
    w1_d = din("mlp_w1", [2, D, 4096])
    w2_d = din("mlp_w2", [2, 4096, D])
    cols_d = din("cols", [128, 88])
    rows_d = din("rows", [3, 2048])
    wsT_d = din("wsT", [128, 8 * 128])
    tabs_d = din("tabs", [ntile, 8, 128, 2 * T])
    out_d = nc.dram_tensor("out", [ntok, D], F32, kind="ExternalOutput").ap()

    def dscr(name, nblk):
        return nc.dram_tensor(name, [nblk, 128, 4096], BF16, kind="Internal").ap()

    s_awin = dscr("s_awin", 8)
    s_awout = dscr("s_awout", 4)
    s_bwin = dscr("s_bwin", 12)
    s_bwout = dscr("s_bwout", 4)
    s_w1 = [dscr("s_w1_%d" % l, 8) for l in range(2)]
    s_w2 = [dscr("s_w2_%d" % l, 8) for l in range(2)]

    with ExitStack() as st:
        E = st.enter_context
        xres_t = E(nc.sbuf_tensor("xres", [128, 8 * T], F32))
        arena_t = E(nc.sbuf_tensor("arena", [128, ARENA_KIB * 512], BF16))
        st32_t = E(nc.sbuf_tensor("st32", [128, 8 * 512], F32))
        stbf_t = E(nc.sbuf_tensor("stbf", [128, 2 * 8 * 512], BF16))
        wslot_t = [E(nc.sbuf_tensor("wslot%d" % i, [128, 4096], BF16)) for i in range(NSLOT)]
        tab_t = [E(nc.sbuf_tensor("tab%d" % i, [128, 2 * T], F32)) for i in range(NTAB)]
        rstd_t = E(nc.sbuf_tensor("rstd_sb", [128, T], F32))
        cols_t = E(nc.sbuf_tensor("cols_sb", [128, 88], F32))
        epsc_t = E(nc.sbuf_tensor("epsc", [128, 1], F32))
        rowA_t = E(nc.sbuf_tensor("rowA", [1, 2048], BF16))
        bsb_t = E(nc.sbuf_tensor("bsb", [128, 1024], F32))
        identf_t = E(nc.sbuf_tensor("identf", [128, 128], F32))
        identb_t = E(nc.sbuf_tensor("identb", [128, 128], BF16))
        ones_t = E(nc.sbuf_tensor("ones_sb", [128, 128], BF16))
        mask4_t = E(nc.sbuf_tensor("mask4", [128, 512], BF16))
        wsTf_t = E(nc.sbuf_tensor("wsTf", [128, 1024], F32))
        vss_t = E(nc.sbuf_tensor("vss", [128, 16], F32))
        vrs_t = E(nc.sbuf_tensor("vrs", [128, 4], F32))
        fss_t = E(nc.sbuf_tensor("fss", [128, 8], F32))
        frs_t = E(nc.sbuf_tensor("frs", [128, 4], F32))
        banks = [E(nc.psum_tensor("bank%d" % i, [128, 512], F32)) for i in range(8)]
        esem = {e: E(nc.semaphore("s_" + e)) for e in Prog.ENGS}
        ds_w = [DSem(E(nc.semaphore("d_w%d" % i))) for i in range(NSLOT)]
        ds_tab = [DSem(E(nc.semaphore("d_t%d" % i))) for i in range(NTAB)]
        ds_x = [DSem(E(nc.semaphore("d_x%d" % i))) for i in range(NCH)]
        ds_o = [DSem(E(nc.semaphore("d_o%d" % i))) for i in range(NCH)]
        ds_c = DSem(E(nc.semaphore("d_c")), group=True)
        ds_g = DSem(E(nc.semaphore("d_g")))
        cast_blocks = [("awin", 8), ("awout", 4), ("w1_0", 8), ("w2_0", 8), ("bwin", 12), ("bwout", 4), ("w1_1", 8), ("w2_1", 8)]
        ds_cast = {(nm, b): DSem(E(nc.semaphore("d_cast_%s_%d" % (nm, b))), group=True) for nm, nb_ in cast_blocks for b in range(nb_)}
        block = E(nc.Block())

        AR = Paged(arena_t, ARENA_KIB * 1024)
        K = 1024

        xres = [V(xres_t[:, k * T:(k + 1) * T], [Buf("xres%d" % k)]) for k in range(8)]
        wslot = [V(wslot_t[i][:], [Buf("wslot%d" % i)]) for i in range(NSLOT)]
        tabv = [V(tab_t[i][:], [Buf("tab%d" % i)]) for i in range(NTAB)]
        rstd = V(rstd_t[:], [Buf("rstd")])
        b_cols = Buf("cols")
        b_const = Buf("const")
        b_rows = Buf("rows")
        b_wsT = Buf("wsT")
        b_vss = Buf("vss")
        b_vrs = Buf("vrs")
        b_fss = Buf("fss")
        b_frs = [Buf("frs%d" % c) for c in range(NCH)]
        st32 = [[V(st32_t[:, (h * 2 + kc) * 512:(h * 2 + kc + 1) * 512], [Buf("st32_%d_%d" % (h, kc))]) for kc in range(2)] for h in range(4)]
        stbf2 = [[[V(stbf_t[:, (par * 8 + h * 2 + kc) * 512:(par * 8 + h * 2 + kc + 1) * 512], [Buf("stbf%d_%d_%d" % (par, h, kc))]) for kc in range(2)] for h in range(4)] for par in range(2)]
        pq = [[Buf("ps%d" % b)] for b in range(8)]

        def PS(b, q0=0, nq=4):
            return V(banks[b][:, q0 * 128:(q0 + nq) * 128], pq[b])

        def PSB(b):
            return banks[b][:].bitcast(BF16)

        bank_ctr = [0]

        def nb():
            b = bank_ctr[0] % 8
            bank_ctr[0] += 1
            return b

        COL_NMIX, COL_NFFN, COL_FIN, COL_BU, COL_GV, COL_HG = 0, 16, 32, 40, 56, 72

        def col(c):
            return cols_t[:, c:c + 1]

        def mm(out, lhsT_ap, rhs_ap, rd, start, stop):
            P.add("pe", lambda e: e.matmul(out.ap, lhsT=lhsT_ap, rhs=rhs_ap, start=start, stop=stop), reads=rd, writes=out.bufs)

        def tr(out_ap, out_bufs, in_ap, rd, ident_ap):
            P.add("pe", lambda e: e.transpose(out=out_ap, in_=in_ap, identity=ident_ap), reads=rd + [b_const], writes=out_bufs)

        def act(out, in_, func, rd=(), bias=None, scale=None, accum=None, extra_w=()):
            kw = {}
            if bias is not None:
                kw["bias"] = bias
            if scale is not None:
                kw["scale"] = scale
            if accum is not None:
                kw["accum_out"] = accum
            P.add("act", lambda e: e.activation(out=out.ap, in_=in_.ap, func=func, **kw), reads=list(in_.bufs) + list(rd), writes=list(out.bufs) + list(extra_w))

        def tt(eng, out, a, b, op):
            P.add(eng, lambda e: e.tensor_tensor(out=out.ap, in0=a.ap, in1=b.ap, op=op), reads=list(a.bufs) + list(b.bufs), writes=out.bufs)

        def stt(eng, out, a, scalar, b, op0, op1, rd=()):
            P.add(eng, lambda e: e.scalar_tensor_tensor(out=out.ap, in0=a.ap, scalar=scalar, in1=b.ap, op0=op0, op1=op1),
                  reads=list(a.bufs) + list(b.bufs) + list(rd), writes=out.bufs)

        def dma(q, out_ap, in_ap, rd, wr, dsem):
            P.add(q, lambda e: e.dma_start(out=out_ap, in_=in_ap), reads=rd, writes=wr, dsem=dsem)

        P.add("pool", lambda e: e.memset(identf_t[:], 1.0), writes=[b_const])
        P.add("pool", lambda e: e.affine_select(out=identf_t[:], in_=identf_t[:], pattern=[[-1, 128]], compare_op=ALU.is_equal,
                                                fill=0.0, base=0, channel_multiplier=1), reads=[b_const], writes=[b_const])
        P.add("pool", lambda e: e.tensor_copy(out=identb_t[:], in_=identf_t[:]), reads=[b_const], writes=[b_const])
        P.add("pool", lambda e: e.memset(ones_t[:], 1.0), writes=[b_const])
        P.add("pool", lambda e: e.memset(epsc_t[:], EPS), writes=[b_const])
        P.add("pool", lambda e: e.memset(mask4_t[:], 1.0), writes=[b_const])
        P.add("pool", lambda e: e.affine_select(out=mask4_t[:].rearrange("p (h i) -> p h i", h=4), in_=mask4_t[:].rearrange("p (h i) -> p h i", h=4),
                                                pattern=[[0, 4], [1, 128]], compare_op=ALU.is_ge, fill=0.0, base=0, channel_multiplier=-1),
              reads=[b_const], writes=[b_const])
        rowsf_ap = arena_t[0:1, 0:12288].bitcast(F32)
        rowsf_b = AR.pages[0:24]
        dma("sp", cols_t[:], cols_d, [], [b_cols], ds_c)
        dma("sp", rowsf_ap, rows_d.rearrange("(o r) n -> o (r n)", o=1), [], rowsf_b, ds_c)
        dma("sp", wsTf_t[:], wsT_d, [], [b_wsT], ds_c)
        b_bsb = Buf("bsb")
        dma("sp", bsb_t[:], rows_d[2:3, 0:1024].partition_broadcast(128), [], [b_bsb], ds_c)
        P.add("dve", lambda e: e.tensor_copy(out=rowA_t[:], in_=rowsf_ap[:, 0:2048]), reads=rowsf_b, writes=[b_rows])

        P.add("pool", lambda e: e.affine_select(out=wsTf_t[:].rearrange("p (g t) -> p g t", g=8), in_=wsTf_t[:].rearrange("p (g t) -> p g t", g=8),
                                                pattern=[[0, 8], [1, 128]], compare_op=ALU.is_ge, fill=0.0, base=0, channel_multiplier=-1),
              reads=[b_wsT], writes=[b_wsT])

        b_scr = {}
        cast_q = []

        def pump_casts(n):
            for _ in range(n):
                if cast_q:
                    o_, i_, w_, d_ = cast_q.pop(0)
                    dma("pool", o_, i_, [], w_, d_)

        def cast_type1(w_ap, scr, ncols, name, b0=0):
            src = w_ap.rearrange("(kc p) (b c) -> b p kc c", p=128, c=512)
            for b in range(ncols // 512):
                bb = Buf("%s_%d" % (name, b0 + b))
                b_scr[(name, b0 + b)] = [bb]
                cast_q.append((scr[b0 + b].rearrange("p (kc c) -> p kc c", c=512), src[b], [bb], ds_cast[(name, b0 + b)]))

        def cast_type2(w_ap, scr, kdim, cb, name):
            nk = kdim // 128
            src = w_ap.rearrange("(kc p) (b c) -> b p kc c", p=128, c=cb)
            for b in range(1024 // cb):
                bb = [Buf("%s_%d_0" % (name, b)), Buf("%s_%d_1" % (name, b))]
                b_scr[(name, b)] = bb
                dst = scr[b].rearrange("p (kc c) -> p kc c", c=cb)
                h = nk // 2
                cast_q.append((dst[:, 0:h, :], src[b][:, 0:h, :], [bb[0]], ds_cast[(name, b)]))
                cast_q.append((dst[:, h:nk, :], src[b][:, h:nk, :], [bb[1]], ds_cast[(name, b)]))

        cast_type1(a_w_in_d, s_awin, 4096, "awin")
        cast_type2(a_w_out_d, s_awout, 2048, 256, "awout")
        cast_type1(w1_d[0], s_w1[0], 4096, "w1_0")
        cast_type2(w2_d[0], s_w2[0], 4096, 128, "w2_0")
        cast_type1(b_w_in_d, s_bwin, 6144, "bwin")
        cast_type1(b_w_out_d[0:1024, :], s_bwout, 1024, "bwout", 0)
        cast_type1(b_w_out_d[1024:2048, :], s_bwout, 1024, "bwout", 2)
        cast_type1(w1_d[1], s_w1[1], 4096, "w1_1")
        cast_type2(w2_d[1], s_w2[1], 4096, 128, "w2_1")
        pump_casts(16)

        wctr = [0]

        def load_w(scr, name, b):
            i = wctr[0] % NSLOT
            wctr[0] += 1
            dma("sp", wslot[i].ap, scr[b], b_scr[(name, b)], wslot[i].bufs, ds_w[i])
            return wslot[i], wslot_t[i]

        xn_off = 0

        def xn(kc, c0=0, n=T):
            return AR.view(xn_off + (kc * T + c0) * 2, n, BF16)

        def rmsnorm_to_xn(colbase):
            P.stage = P.stage.split(":")[0] + ":norm%d" % colbase
            sqv = [AR.view(8 * K + kc * T * 2, T, BF16) for kc in range(8)]
            xg = [AR.view(64 * K + i * 2048, T, F32) for i in range(4)]
            for kc in range(8):
                if colbase == COL_NMIX and kc % 2 == 0:
                    tt("pool", sqv[kc], xres[kc], xres[kc], ALU.mult)
                else:
                    act(sqv[kc], xres[kc], AF.Square)
                if kc % 2 == 1:
                    act(xg[kc // 2], xres[kc], AF.Copy, rd=[b_cols], scale=col(colbase + kc))
            b = nb()
            for kc in range(8):
                mm(PS(b), ones_t[:], sqv[kc].ap, sqv[kc].bufs + [b_const], kc == 0, kc == 7)
            act(rstd, PS(b), AF.Sqrt, rd=[b_const], bias=epsc_t[:, 0:1], scale=1.0 / D)
            P.add("dve", lambda e: e.reciprocal(out=rstd.ap, in_=rstd.ap), reads=rstd.bufs, writes=rstd.bufs)
            for kc in range(8):
                if kc % 2 == 0:
                    stt("dve", xn(kc), xres[kc], col(colbase + kc), rstd, ALU.mult, ALU.mult, rd=[b_cols])
                else:
                    tt("pool", xn(kc), xg[kc // 2], rstd, ALU.mult)

        def load_x_tile(tok0):
            xin = [AR.view(16 * K + c * 4096, 1024, F32) for c in range(NCH)]
            for c in range(NCH):
                dma("sp", xin[c].ap, x_d[tok0 + c * 128: tok0 + (c + 1) * 128, :], [], xin[c].bufs, ds_x[c])
            for kc in range(8):
                b = nb()
                for c in range(NCH):
                    o = PS(b, c, 1)
                    tr(o.ap, o.bufs, xin[c].ap[:, kc * 128:(kc + 1) * 128], xin[c].bufs, identf_t[:])
                if kc % 2 == 0:
                    act(xres[kc], PS(b), AF.Copy)
                else:
                    P.add("dve", (lambda kc, b: lambda e: e.tensor_copy(out=xres[kc].ap, in_=banks[b][:]))(kc, b), reads=pq[b], writes=xres[kc].bufs)

        def mixer_a():
            rmsnorm_to_xn(COL_NMIX + 0)
            pump_casts(36)
            u = [AR.view(16 * K + fc * T * 2, T, BF16) for fc in range(16)]
            vbf = [AR.view(32 * K + c * 4096, 2048, BF16) for c in range(NCH)]
            wTs = [AR.view(48 * K + i * 2048, 1024, BF16) for i in range(4)]
            junk = AR.view(56 * K, 512, BF16)
            sgt = [AR.view(58 * K + i * 1024, 128, F32) for i in range(4)]
            P.stage = P.stage.split(":")[0] + ":A_v"
            b_vssc = [Buf("vss%d" % c) for c in range(NCH)]
            b_vrsc = [Buf("vrs%d" % c) for c in range(NCH)]
            P.add("pool", lambda e: e.memset(vss_t[:], 0.0), writes=b_vssc)
            for vb in range(4):
                ws, wt = load_w(s_awin, "awin", 4 + vb)
                for c in range(NCH):
                    b = nb()
                    mm(PS(b), ones_t[0:1, :], rowA_t[0:1, vb * 512:(vb + 1) * 512], [b_const, b_rows], True, False)
                    for kc in range(8):
                        xv = xn(kc, c * 128, 128)
                        mm(PS(b), xv.ap, wt[:, kc * 512:(kc + 1) * 512], ws.bufs + xv.bufs, False, kc == 7)
                    vv = V(vbf[c].ap[:, vb * 512:(vb + 1) * 512], vbf[c].bufs)
                    act(vv, PS(b), AF.Gelu_apprx_tanh)
                    act(junk, vv, AF.Square, accum=vss_t[:, c * 4 + vb: c * 4 + vb + 1], extra_w=[b_vssc[c]])
            for c in range(NCH):
                P.add("dve", (lambda c: lambda e: e.tensor_reduce(out=vrs_t[:, c:c + 1], in_=vss_t[:, c * 4:(c + 1) * 4], axis=mybir.AxisListType.X, op=ALU.add))(c),
                      reads=[b_vssc[c]], writes=[b_vrsc[c]])
                P.add("act", (lambda c: lambda e: e.activation(out=vrs_t[:, c:c + 1], in_=vrs_t[:, c:c + 1], func=AF.Sqrt, bias=epsc_t[:, 0:1], scale=1.0 / 2048))(c),
                      reads=[b_vrsc[c], b_const], writes=[b_vrsc[c]])
                P.add("dve", (lambda c: lambda e: e.reciprocal(out=vrs_t[:, c:c + 1], in_=vrs_t[:, c:c + 1]))(c), reads=[b_vrsc[c]], writes=[b_vrsc[c]])
                P.add("act", (lambda w_c, c: lambda e: e.activation(out=w_c.ap, in_=wsTf_t[:], func=AF.Copy, scale=vrs_t[:, c:c + 1]))(wTs[c], c),
                      reads=[b_wsT, b_vrsc[c]], writes=wTs[c].bufs)

            def sgate_group(gq):
                for c in range(NCH):
                    w_c = wTs[c]
                    b = nb()
                    for r in range(4):
                        fc = gq * 4 + r
                        g = fc // 2
                        o = PS(b, r, 1)
                        mm(o, vbf[c].ap[:, fc * 128:(fc + 1) * 128], w_c.ap[:, g * 128:(g + 1) * 128], vbf[c].bufs + w_c.bufs, True, True)
                    for r in range(4):
                        fc = gq * 4 + r
                        g = fc // 2
                        o = PS(b, r, 1)
                        uu = V(u[fc].ap[:, c * 128:(c + 1) * 128], u[fc].bufs)
                        st_ = sgt[r]
                        stt("dve", st_, o, col(COL_GV + fc), V(bsb_t[:, g * 128:(g + 1) * 128], [b_bsb]), ALU.mult, ALU.add, rd=[b_cols])
                        tt("pool", uu, st_, uu, ALU.mult)

            for blk in range(4):
                P.stage = P.stage.split(":")[0] + ":A_u"
                ws, wt = load_w(s_awin, "awin", blk)
                for m in range(4):
                    b = nb()
                    for kc in range(8):
                        mm(PS(b), wt[:, kc * 512 + m * 128: kc * 512 + (m + 1) * 128], xn(kc).ap, ws.bufs + xn(kc).bufs, kc == 0, kc == 7)
                    fc = blk * 4 + m
                    act(u[fc], PS(b), AF.Gelu_apprx_tanh, rd=[b_cols], bias=col(COL_BU + fc))
                if blk >= 1:
                    P.stage = P.stage.split(":")[0] + ":A_sg"
                    sgate_group(blk - 1)
            P.stage = P.stage.split(":")[0] + ":A_sg"
            sgate_group(3)
            P.stage = P.stage.split(":")[0] + ":A_out"
            w01 = [load_w(s_awout, "awout", 0), load_w(s_awout, "awout", 1)]
            b01 = [nb() for _ in range(4)]
            for lo, hi in ((0, 12), (12, 16)):
                for m in range(4):
                    ws, wt = w01[m // 2]
                    mmi = m % 2
                    for kc in range(lo, hi):
                        mm(PS(b01[m]), wt[:, kc * 256 + mmi * 128: kc * 256 + (mmi + 1) * 128], u[kc].ap, ws.bufs + u[kc].bufs, kc == 0, kc == 15)
                    if hi == 16:
                        tt("dve", xres[m], xres[m], PS(b01[m]), ALU.add)
            for blk in range(2, 4):
                ws, wt = load_w(s_awout, "awout", blk)
                for mmi in range(2):
                    m = blk * 2 + mmi
                    b = nb()
                    for kc in range(16):
                        mm(PS(b), wt[:, kc * 256 + mmi * 128: kc * 256 + (mmi + 1) * 128], u[kc].ap, ws.bufs + u[kc].bufs, kc == 0, kc == 15)
                    tt("dve", xres[m], xres[m], PS(b), ALU.add)

        def mlp(l):
            colbase = COL_NFFN + 8 * l
            P.stage = P.stage.split(":")[0] + ":norm%d" % colbase
            sqv = [AR.view(8 * K + kc * T * 2, T, BF16) for kc in range(8)]
            for kc in range(8):
                act(xn(kc), xres[kc], AF.Copy, rd=[b_cols], scale=col(colbase + kc))
            for kc in range(8):
                act(sqv[kc], xres[kc], AF.Square)
            b = nb()
            for kc in range(8):
                mm(PS(b), ones_t[:], sqv[kc].ap, sqv[kc].bufs + [b_const], kc == 0, kc == 7)
            act(rstd, PS(b), AF.Sqrt, rd=[b_const], bias=epsc_t[:, 0:1], scale=1.0 / D)
            P.add("dve", lambda e: e.reciprocal(out=rstd.ap, in_=rstd.ap), reads=rstd.bufs, writes=rstd.bufs)
            H = [AR.view(32 * K + fc * T * 2, T, BF16) for fc in range(32)]
            rt = [AR.view(64 * K + i * 2048, T, F32) for i in range(4)]
            ri = 0
            P.stage = P.stage.split(":")[0] + ":m%d_w1" % l
            for blk in range(8):
                ws, wt = load_w(s_w1[l], "w1_%d" % l, blk)
                for m in range(4):
                    b = nb()
                    for kc in range(8):
                        mm(PS(b), wt[:, kc * 512 + m * 128: kc * 512 + (m + 1) * 128], xn(kc).ap, ws.bufs + xn(kc).bufs, kc == 0, kc == 7)
                    r = rt[ri % 4]
                    ri += 1
                    stt("dve", r, PS(b), 0.0, rstd, ALU.max, ALU.mult)
                    tt("pool", H[blk * 4 + m], r, r, ALU.mult)
                    if blk >= 1:
                        pump_casts(2)
            P.stage = P.stage.split(":")[0] + ":m%d_w2" % l
            for m in range(8):
                ws, wt = load_w(s_w2[l], "w2_%d" % l, m)
                b = nb()
                for kc in range(32):
                    mm(PS(b), wt[:, kc * 128:(kc + 1) * 128], H[kc].ap, ws.bufs + H[kc].bufs, kc == 0, kc == 31)
                tt("dve", xres[m], xres[m], PS(b), ALU.add)

        tabctr = [0]

        def mixer_b(tile_idx):
            rmsnorm_to_xn(COL_NMIX + 8)
            qT = [AR.view(16 * K + i * T * 2, T, BF16) for i in range(8)]
            kT = [AR.view(24 * K + i * T * 2, T, BF16) for i in range(8)]
            ktok = [AR.view(32 * K + c * 2048, 1024, BF16) for c in range(NCH)]
            vtok = [AR.view(40 * K + c * 4096, 2048, BF16) for c in range(NCH)]
            sg = [AR.view(56 * K + fc * T * 2, T, BF16) for fc in range(16)]
            rtmp = [AR.view(64 * K + i * 2048, T, F32) for i in range(4)]
            yoff = [8 * K + i * 1024 for i in range(8)] + [16 * K + i * 1024 for i in range(4)] + [24 * K + i * 1024 for i in range(4)]
            yT = [AR.view(yoff[fc], T, BF16) for fc in range(16)]
            PT = [AR.view(72 * K + i * 1024, 512, BF16) for i in range(2)]
            sqh = [AR.view(74 * K + i * 1024, 512, BF16) for i in range(2)]
            rs = [AR.view(76 * K + i * 2048, 512, F32) for i in range(2)]
            y1h = [AR.view(80 * K + i * 2048, 512, F32) for i in range(2)]
            sgtmp = [AR.view(84 * K + i * 2048, 512, F32) for i in range(2)]
            fbank = [0]
            ubank = [0]

            def bank_free():
                return nb()

            def bank_fill():
                fbank[0] ^= 1
                return 6 + fbank[0]

            def rotary(isk, h, b1, b2):
                dst = kT if isk else qT
                ti = tabctr[0] % NTAB
                tabctr[0] += 1
                dma("pool", tabv[ti].ap, tabs_d[tile_idx, isk * 4 + h], [], tabv[ti].bufs, ds_tab[ti])
                Cd = V(tab_t[ti][:, 0:T], tabv[ti].bufs)
                Sd = V(tab_t[ti][:, T:2 * T], tabv[ti].bufs)
                t1 = PS(b1)
                t2 = PS(b2)
                tt("dve", rtmp[0], t1, Cd, ALU.mult)
                tt("dve", rtmp[1], t2, Sd, ALU.mult)
                tt("pool", dst[h * 2 + 0], rtmp[0], rtmp[1], ALU.subtract)
                tt("dve", rtmp[2], t2, Cd, ALU.mult)
                tt("dve", rtmp[3], t1, Sd, ALU.mult)
                tt("pool", dst[h * 2 + 1], rtmp[2], rtmp[3], ALU.add)

            def units_qk(blk, balloc):
                st_ = {}

                def unit(m):
                    def f():
                        if m == 0:
                            st_["w"] = load_w(s_bwin, "bwin", blk)
                            st_["b"] = []
                        ws, wt = st_["w"]
                        b = balloc()
                        st_["b"].append(b)
                        for kc in range(8):
                            mm(PS(b), wt[:, kc * 512 + m * 128: kc * 512 + (m + 1) * 128], xn(kc).ap, ws.bufs + xn(kc).bufs, kc == 0, kc == 7)
                        if m % 2 == 1:
                            rotary(blk // 2, (blk % 2) * 2 + m // 2, st_["b"][m - 1], st_["b"][m])
                    return f
                return [unit(m) for m in range(4)]

            def units_v(vb, balloc):
                st_ = {}

                def unit(c):
                    def f():
                        if c == 0:
                            st_["w"] = load_w(s_bwin, "bwin", 4 + vb)
                        ws, wt = st_["w"]
                        b = balloc()
                        for kc in range(8):
                            xv = xn(kc, c * 128, 128)
                            mm(PS(b), xv.ap, wt[:, kc * 512:(kc + 1) * 512], ws.bufs + xv.bufs, kc == 0, kc == 7)
                        act(V(vtok[c].ap[:, vb * 512:(vb + 1) * 512], vtok[c].bufs), PS(b), AF.Copy)
                    return f
                return [unit(c) for c in range(NCH)]

            def units_g(gb, balloc):
                st_ = {}

                def unit(m):
                    def f():
                        if m == 0:
                            st_["w"] = load_w(s_bwin, "bwin", 8 + gb)
                        ws, wt = st_["w"]
                        b = balloc()
                        for kc in range(8):
                            mm(PS(b), wt[:, kc * 512 + m * 128: kc * 512 + (m + 1) * 128], xn(kc).ap, ws.bufs + xn(kc).bufs, kc == 0, kc == 7)
                        fc = gb * 4 + m
                        tmp_ = sgtmp[fc % 2]
                        act(tmp_, PS(b), AF.Silu)
                        act(sg[fc], tmp_, AF.Copy, rd=[b_cols], scale=col(COL_HG + fc))
                    return f
                return [unit(m) for m in range(4)]

            def transposes(pair, balloc):
                for c in range(NCH):
                    b = balloc()
                    pb = PSB(b)
                    for j in range(4):
                        i = pair * 4 + j
                        tr(pb[:, j * 128:(j + 1) * 128], pq[b], kT[i].ap[:, c * 128:(c + 1) * 128], kT[i].bufs, identb_t[:])
                    P.add("act", (lambda kt, pb, pair: lambda e: e.activation(out=kt.ap[:, pair * 512:(pair + 1) * 512], in_=pb[:, 0:512], func=AF.Copy))(ktok[c], pb, pair),
                          reads=pq[b], writes=ktok[c].bufs)

            def st_S(pair, c):
                cs = slice(c * 128, (c + 1) * 128)
                for s_ in range(2):
                    h = pair * 2 + s_
                    for half in range(2):
                        i = h * 2 + half
                        mm(PS(0, s_, 1), kT[i].ap[:, cs], qT[i].ap[:, cs], kT[i].bufs + qT[i].bufs, half == 0, half == 1)
                pt = PT[c % 2]
                P.add("dve", (lambda pt: lambda e: e.tensor_tensor(out=pt.ap[:, 0:256], in0=banks[0][:, 0:256], in1=mask4_t[:, 0:256], op=ALU.mult))(pt),
                      reads=pq[0] + [b_const], writes=pt.bufs)

            def st_O(pair, c, s_):
                h = pair * 2 + s_
                cs = slice(c * 128, (c + 1) * 128)
                pt = PT[c % 2]
                b = 4 + s_
                for vc in range(4):
                    o = PS(b, vc, 1)
                    mm(o, vtok[c].ap[:, h * 512 + vc * 128: h * 512 + (vc + 1) * 128], pt.ap[:, s_ * 128:(s_ + 1) * 128], vtok[c].bufs + pt.bufs, True, False)
                    for kc in range(2):
                        sb_ = stbf2[c % 2][h][kc]
                        mm(o, sb_.ap[:, vc * 128:(vc + 1) * 128], qT[h * 2 + kc].ap[:, cs], sb_.bufs + qT[h * 2 + kc].bufs, False, kc == 1)
                act(sqh[s_], PS(b), AF.Square)

            def st_N(pair, c, s_):
                sq_ = sqh[s_]
                for vc in range(4):
                    mm(PS(0, 2 + s_, 1), ones_t[:], sq_.ap[:, vc * 128:(vc + 1) * 128], sq_.bufs + [b_const], vc == 0, vc == 3)
                if s_ == 0:
                    return
                r2 = V(rs[c % 2].ap[:, 0:256], rs[c % 2].bufs)
                act(r2, PS(0, 2, 2), AF.Sqrt, rd=[b_const], bias=epsc_t[:, 0:1], scale=1.0 / 512)
                P.add("dve", (lambda r2: lambda e: e.reciprocal(out=r2.ap, in_=r2.ap))(r2), reads=r2.bufs, writes=r2.bufs)
                for ss in range(2):
                    h = pair * 2 + ss
                    b = 4 + ss
                    yh = y1h[ss]
                    y4 = yh.ap.rearrange("p (v i) -> p v i", v=4)
                    o4 = banks[b][:].rearrange("p (v i) -> p v i", v=4)
                    rb = r2.ap[:, ss * 128:(ss + 1) * 128].unsqueeze(1).to_broadcast([128, 4, 128])
                    P.add("dve", (lambda y4, o4, rb: lambda e: e.tensor_tensor(out=y4, in0=o4, in1=rb, op=ALU.mult))(y4, o4, rb), reads=pq[b] + r2.bufs, writes=yh.bufs)
                    fc0 = h * 4
                    sg4 = arena_t[:, (56 * K + fc0 * 1024) // 2:(56 * K + (fc0 + 4) * 1024) // 2].rearrange("p (v t) -> p v t", v=4)[:, :, c * 128:(c + 1) * 128]
                    yo4 = arena_t[:, yoff[fc0] // 2:(yoff[fc0] + 4 * 1024) // 2].rearrange("p (v t) -> p v t", v=4)[:, :, c * 128:(c + 1) * 128]
                    sgb = [bb for fc in range(fc0, fc0 + 4) for bb in sg[fc].bufs]
                    yob = [bb for fc in range(fc0, fc0 + 4) for bb in yT[fc].bufs]
                    P.add("pool", (lambda yo4, y4, sg4: lambda e: e.tensor_tensor(out=yo4, in0=y4, in1=sg4, op=ALU.mult))(yo4, y4, sg4), reads=yh.bufs + sgb, writes=yob)

            def st_U(pair, c, s_):
                h = pair * 2 + s_
                gC = float(np.float32(1.0 - 2.0 ** (-5.0 - h)) ** 128)
                for kc in range(2):
                    bd = 1 + (ubank[0] % 3)
                    ubank[0] += 1
                    mm(PS(bd), ktok[c].ap[:, h * 256 + kc * 128: h * 256 + (kc + 1) * 128], vtok[c].ap[:, h * 512:(h + 1) * 512], ktok[c].bufs + vtok[c].bufs, True, True)
                    s32 = st32[h][kc]
                    P.add("dve", (lambda s32, bd, gC: lambda e: e.scalar_tensor_tensor(out=s32.ap, in0=s32.ap, scalar=gC, in1=banks[bd][:], op0=ALU.mult, op1=ALU.add))(s32, bd, gC),
                          reads=s32.bufs + pq[bd], writes=s32.bufs)
                    act(stbf2[(c + 1) % 2][h][kc], s32, AF.Copy, scale=gC)

            def retention(pair, fillers, front=0):
                steps = [lambda: st_S(pair, 0)]
                for c in range(NCH):
                    steps.append((lambda c: lambda: st_U(pair, c, 0))(c))
                    steps.append((lambda c: lambda: st_O(pair, c, 0))(c))
                    steps.append((lambda c: lambda: st_U(pair, c, 1))(c))
                    steps.append((lambda c: lambda: st_O(pair, c, 1))(c))
                    if c + 1 < NCH:
                        steps.append((lambda c: lambda: st_S(pair, c + 1))(c))
                    steps.append((lambda c: lambda: st_N(pair, c, 0))(c))
                    steps.append((lambda c: lambda: st_N(pair, c, 1))(c))
                nf = len(fillers)
                done = 0
                for i, stp in enumerate(steps):
                    stp()
                    if front:
                        want = min(front, 2 * (i + 1)) if 2 * (i + 1) <= front + 1 else front + ((i + 1 - front // 2) * (nf - front) + len(steps) - front // 2 - 1) // (len(steps) - front // 2)
                    else:
                        want = ((i + 1) * nf + len(steps) - 1) // len(steps)
                    while done < min(want, nf):
                        fillers[done]()
                        done += 1
                while done < nf:
                    fillers[done]()
                    done += 1

            def units_wo(half, balloc):
                us = []
                for j in range(2):
                    st_ = {}

                    def unit(j, mmi, st_=st_):
                        def f():
                            if mmi == 0:
                                st_["w"] = load_w(s_bwout, "bwout", half * 2 + j)
                            ws, wt = st_["w"]
                            m = j * 4 + mmi
                            b = balloc()
                            for kc in range(8):
                                yk = yT[half * 8 + kc]
                                mm(PS(b), wt[:, kc * 512 + mmi * 128: kc * 512 + (mmi + 1) * 128], yk.ap, ws.bufs + yk.bufs, kc == 0, kc == 7)
                            tt("dve", xres[m], xres[m], PS(b), ALU.add)
                        return f
                    us += [unit(j, mmi) for mmi in range(4)]
                return us

            P.stage = P.stage.split(":")[0] + ":B_pre"
            for u_ in units_qk(0, bank_free) + units_qk(2, bank_free) + units_v(0, bank_free) + units_v(1, bank_free) + units_g(0, bank_free) + units_g(1, bank_free):
                u_()
            P.stage = P.stage.split(":")[0] + ":B_retA"
            transposes(0, bank_free)
            retention(0, units_qk(1, bank_fill) + units_qk(3, bank_fill) + units_v(2, bank_fill) + units_v(3, bank_fill))
            P.stage = P.stage.split(":")[0] + ":B_retB"
            transposes(1, bank_fill)
            retention(1, units_g(2, bank_fill) + units_g(3, bank_fill) + units_wo(0, bank_fill), front=8)
            P.stage = P.stage.split(":")[0] + ":B_out2"
            for u_ in units_wo(1, bank_free):
                u_()

        def store_tile(tok0, do_norm):
            P.stage = P.stage.split(":")[0] + ":fin"
            ost = [AR.view(48 * K + c * 4096, 1024, F32) for c in range(NCH)]
            if do_norm:
                gbc = AR.view(32 * K, 1024, F32)
                fjunk = AR.view(36 * K, 512, BF16)
                dma("pool", gbc.ap, rows_d[2:3, 1024:2048].partition_broadcast(128), [], gbc.bufs, ds_g)
                P.add("pool", lambda e: e.memset(fss_t[:], 0.0), writes=[b_fss])
            for c in range(NCH):
                bh = [nb(), nb()]
                for half in range(2):
                    for j in range(4):
                        kc = half * 4 + j
                        o = PS(bh[half], j, 1)
                        tr(o.ap, o.bufs, xres[kc].ap[:, c * 128:(c + 1) * 128], xres[kc].bufs, identf_t[:])
                if do_norm:
                    for half in range(2):
                        act(fjunk, PS(bh[half]), AF.Square, accum=fss_t[:, c * 2 + half: c * 2 + half + 1], extra_w=[b_fss])
                    P.add("dve", (lambda c: lambda e: e.tensor_reduce(out=frs_t[:, c:c + 1], in_=fss_t[:, c * 2:c * 2 + 2], axis=mybir.AxisListType.X, op=ALU.add))(c),
                          reads=[b_fss], writes=[b_frs[c]])
                    P.add("act", (lambda c: lambda e: e.activation(out=frs_t[:, c:c + 1], in_=frs_t[:, c:c + 1], func=AF.Sqrt, bias=epsc_t[:, 0:1], scale=1.0 / D))(c),
                          reads=[b_frs[c], b_const], writes=[b_frs[c]])
                    P.add("dve", (lambda c: lambda e: e.reciprocal(out=frs_t[:, c:c + 1], in_=frs_t[:, c:c + 1]))(c), reads=[b_frs[c]], writes=[b_frs[c]])
                    for half in range(2):
                        stt("dve", V(ost[c].ap[:, half * 512:(half + 1) * 512], ost[c].bufs), PS(bh[half]), frs_t[:, c:c + 1],
                            V(gbc.ap[:, half * 512:(half + 1) * 512], gbc.bufs), ALU.mult, ALU.mult, rd=[b_frs[c]])
                else:
                    for half in range(2):
                        act(V(ost[c].ap[:, half * 512:(half + 1) * 512], ost[c].bufs), PS(bh[half]), AF.Copy)
                dma("act", out_d[tok0 + c * 128: tok0 + (c + 1) * 128, :], ost[c].ap, ost[c].bufs, [], ds_o[c])

        for s in range(nseq):
            for h in range(4):
                for kc in range(2):
                    P.add("pool", (lambda v: lambda e: e.memset(v.ap, 0.0))(st32[h][kc]), writes=st32[h][kc].bufs)
                    P.add("pool", (lambda v: lambda e: e.memset(v.ap, 0.0))(stbf2[0][h][kc]), writes=stbf2[0][h][kc].bufs)
            for ti in range(ntile):
                tok0 = s * seqlen + ti * T
                P.stage = "t%d:ldx" % (s * ntile + ti)
                load_x_tile(tok0)
                if dbg_stop == "load":
                    store_tile(tok0, False)
                    continue
                mixer_a()
                if dbg_stop == "mixa":
                    store_tile(tok0, False)
                    continue
                mlp(0)
                if dbg_stop == "mlp0":
                    store_tile(tok0, False)
                    continue
                mixer_b(ti)
                if dbg_stop == "mixb":
                    store_tile(tok0, False)
                    continue
                mlp(1)
                if dbg_stop == "mlp1":
                    store_tile(tok0, False)
                    continue
                store_tile(tok0, True)

        counts = P.emit(nc, block, esem)
        import os
        if os.environ.get("MK_STAGE_LOG"):
            import json
            json.dump(P.stage_log, open(os.environ["MK_STAGE_LOG"], "w"))
    return nc, counts


def _tables(seqlen):
    f32 = np.float32
    pos = np.arange(seqlen, dtype=f32)
    inv_freq = (1.0 / (f32(10000.0) ** np.linspace(0.0, 1.0, 128, dtype=f32))).astype(f32)
    ang = (pos[:, None] * inv_freq[None, :]).astype(f32)
    cos = np.cos(ang).astype(f32).T
    sin = np.sin(ang).astype(f32).T
    idx = (np.arange(seqlen) % 128).astype(f32)
    ntile = seqlen // T
    tabs = np.zeros((ntile, 8, 128, 2 * T), dtype=f32)
    for h in range(4):
        lg = np.log(f32(1.0) - f32(2.0) ** f32(-5.0 - h)).astype(f32)
        qd = np.exp(lg * (idx + 1.0)).astype(f32)
        kd = (np.exp(-lg * (idx + 1.0)) * f32(256.0 ** -0.5)).astype(f32)
        for isk, dd in ((0, qd), (1, kd)):
            C = (cos * dd[None, :]).astype(f32).reshape(128, ntile, T)
            S_ = (sin * dd[None, :]).astype(f32).reshape(128, ntile, T)
            tabs[:, isk * 4 + h, :, 0:T] = C.transpose(1, 0, 2)
            tabs[:, isk * 4 + h, :, T:2 * T] = S_.transpose(1, 0, 2)
    return tabs


def _colize(v):
    v = np.asarray(v, dtype=np.float32)
    return v.reshape(-1, 128).T


def _prep_shared(inp, seqlen):
    cols = np.concatenate([
        _colize(inp["norm_mix_g"][0]), _colize(inp["norm_mix_g"][1]),
        _colize(inp["norm_ffn_g"][0]), _colize(inp["norm_ffn_g"][1]),
        _colize(inp["final_norm_g"]),
        _colize(inp["a_b_in"][0][:2048]),
        _colize(inp["a_v_norm_g"][0]),
        _colize(inp["b_head_norm_g"][0]),
    ], axis=1).astype(np.float32)
    assert cols.shape == (128, 88)
    rows = np.zeros((3, 2048), dtype=np.float32)
    rows[0] = inp["a_b_in"][0][2048:]
    rows[1] = inp["a_v_norm_g"][0]
    rows[2, :1024] = np.asarray(inp["a_b_s"][0]).reshape(-1)
    rows[2, 1024:] = np.asarray(inp["final_norm_g"]).reshape(-1)
    wsT = np.ascontiguousarray(np.asarray(inp["a_w_s"][0]).transpose(2, 0, 1)).reshape(128, 1024)
    shared = {
        "a_w_in": np.ascontiguousarray(inp["a_w_in"][0]),
        "a_w_out": np.ascontiguousarray(inp["a_w_out"][0]),
        "b_w_in": np.ascontiguousarray(inp["b_w_in"][0]),
        "b_w_out": np.ascontiguousarray(inp["b_w_out"][0]),
        "mlp_w1": np.ascontiguousarray(inp["mlp_w1"]),
        "mlp_w2": np.ascontiguousarray(inp["mlp_w2"]),
        "cols": np.ascontiguousarray(cols),
        "rows": rows,
        "wsT": wsT,
        "tabs": _tables(seqlen),
    }
    return shared


_CACHE = {}


def run(inp, ncores, nseq, seqlen, dbg_stop=None):
    inp = {k: np.asarray(v, dtype=np.float32) for k, v in inp.items()}
    key = (nseq, seqlen, dbg_stop)
    if key not in _CACHE:
        _CACHE[key] = build_program(nseq, seqlen, dbg_stop)
    nc, counts = _CACHE[key]
    shared = _prep_shared(inp, seqlen)
    x = inp["x"]
    in_maps = []
    for i in range(ncores):
        m = dict(shared)
        m["x"] = np.ascontiguousarray(x[i * nseq:(i + 1) * nseq].reshape(nseq * seqlen, D))
        in_maps.append(m)
    res = run_bass_kernel_spmd(nc, in_maps, core_ids=list(range(ncores)))
    outs = [np.asarray(r["out"]).reshape(nseq, seqlen, D) for r in res.results]
    return np.concatenate(outs, axis=0).astype(np.float32)


def kernel(**inputs):
    return run(inputs, 8, 2, 4096)
```

```python
import numpy as np
from contextlib import ExitStack
import concourse.bass as bass
import concourse.mybir as mybir
from concourse.bass_utils import run_bass_kernel_spmd

F32 = mybir.dt.float32
BF16 = mybir.dt.bfloat16
ALU = mybir.AluOpType
AF = mybir.ActivationFunctionType

D = 1024
T = 512
NCH = 4
NSLOT = 4
NTAB = 2
ARENA_KIB = 88
EPS = 1e-6


class Buf:
    __slots__ = ("name", "lw", "rd")

    def __init__(self, name):
        self.name = name
        self.lw = None
        self.rd = []


class DSem:
    def __init__(self, sem, group=False):
        self.sem = sem
        self.total = 0
        self.group = group


class Op:
    __slots__ = ("eng", "fn", "deps", "marked", "token", "dsem", "idx", "stage")


class V:
    __slots__ = ("ap", "bufs")

    def __init__(self, ap, bufs):
        self.ap = ap
        self.bufs = bufs


class Prog:
    ENGS = ["pe", "act", "dve", "pool", "sp"]

    def __init__(self):
        self.ops = []
        self.stage = "setup"

    def add(self, eng, fn, reads=(), writes=(), dsem=None):
        op = Op()
        op.eng = eng
        op.fn = fn
        op.dsem = dsem
        op.idx = len(self.ops)
        op.marked = False
        op.token = None
        op.stage = self.stage
        deps = set()
        for b in reads:
            if b.lw is not None:
                deps.add(b.lw)
        for b in writes:
            if b.lw is not None:
                deps.add(b.lw)
            deps.update(b.rd)
        for b in writes:
            b.lw = op.idx
            b.rd = []
        for b in reads:
            if b.lw != op.idx:
                b.rd.append(op.idx)
        deps.discard(op.idx)
        op.deps = deps
        self.ops.append(op)
        return op

    def emit(self, nc, block, esem):
        ops = self.ops
        for op in ops:
            for d in op.deps:
                dop = ops[d]
                if dop.dsem is None:
                    if dop.eng == "pe" and op.eng == "pe" and op.dsem is None:
                        continue
                    dop.marked = True
        cnt = {e: 0 for e in self.ENGS}
        for op in ops:
            if op.dsem is not None:
                op.dsem.total += 16
                op.token = (op.dsem.sem, op.dsem.total)
            elif op.marked:
                cnt[op.eng] += 1
                op.token = (esem[op.eng], cnt[op.eng])
        for op in ops:
            if op.dsem is not None and op.dsem.group:
                op.token = (op.dsem.sem, op.dsem.total)
        per = {e: [] for e in self.ENGS}
        for op in ops:
            per[op.eng].append(op)
        stage_log = self.stage_log = []
        final_tokens = {}
        for op in ops:
            if op.token is not None:
                final_tokens[id(op.token[0])] = op.token

        def runner(engname):
            def run(e):
                seen = {}
                pos = 0
                for op in per[engname]:
                    waits = {}
                    for d in op.deps:
                        dop = ops[d]
                        if dop.token is None:
                            continue
                        if dop.dsem is None and op.dsem is None and dop.eng == "pe" and engname == "pe":
                            continue
                        s, v = dop.token
                        k = id(s)
                        if k not in waits or waits[k][1] < v:
                            waits[k] = (s, v)
                    for k, (s, v) in waits.items():
                        if seen.get(k, 0) >= v:
                            continue
                        seen[k] = v
                        e.wait_ge(s, v)
                        pos += 1
                    ins = op.fn(e)
                    if engname == "pe" and op.dsem is None:
                        pos += 1
                    stage_log.append((engname, pos, op.stage))
                    pos += 1
                    if op.dsem is not None:
                        ins.then_inc(op.dsem.sem, 16)
                    elif op.marked:
                        ins.then_inc(esem[engname], 1)
                if engname == "sp":
                    for k, (s, v) in final_tokens.items():
                        if seen.get(k, 0) >= v:
                            continue
                        e.wait_ge(s, v)
            return run

        block.tensor(runner("pe"))
        block.scalar(runner("act"))
        block.vector(runner("dve"))
        block.gpsimd(runner("pool"))
        block.sync(runner("sp"))
        return {e: len(per[e]) for e in self.ENGS}


class Paged:
    def __init__(self, t, nbytes, page=1024, name="ar"):
        self.t = t
        self.page = page
        self.pages = [Buf("%s%d" % (name, i)) for i in range((nbytes + page - 1) // page)]
        self.nbytes = nbytes

    def view(self, off, n, dt):
        sz = 4 if dt == F32 else 2
        assert off % sz == 0 and off + n * sz <= self.nbytes, (off, n, self.nbytes)
        ap = self.t[:, off // 2:(off + n * sz) // 2]
        if dt == F32:
            ap = ap.bitcast(F32)
        p0 = off // self.page
        p1 = (off + n * sz - 1) // self.page
        return V(ap, self.pages[p0:p1 + 1])


def build_program(nseq, seqlen, dbg_stop=None):
    ntile = seqlen // T
    ntok = nseq * seqlen
    nc = bass.Bass("TRN2", target_bir_lowering=False)
    P = Prog()

    def din(name, shape, dt=F32):
        return nc.dram_tensor(name, list(shape), dt, kind="ExternalInput").ap()

    x_d = din("x", [ntok, D])
    a_w_in_d = din("a_w_in", [D, 4096])
    a_w_out_d = din("a_w_out", [2048, D])
    b_w_in_d = din("b_w_in", [D, 6144])
    b_w_out_d = din("b_w_out", [2048, D])
    w1_d = din("mlp_w1", [2, D, 4096])
    w2_d = din("mlp_w2", [2, 4096, D])
    cols_d = din("cols", [128, 88])
    rows_d = din("rows", [3, 2048])
    wsT_d = din("wsT", [128, 8 * 128])
    tabs_d = din("tabs", [ntile, 8, 128, 2 * T])
    out_d = nc.dram_tensor("out", [ntok, D], F32, kind="ExternalOutput").ap()

    def dscr(name, nblk):
        return nc.dram_tensor(name, [nblk, 128, 4096], BF16, kind="Internal").ap()

    s_awin = dscr("s_awin", 8)
    s_awout = dscr("s_awout", 4)
    s_bwin = dscr("s_bwin", 12)
    s_bwout = dscr("s_bwout", 4)
    s_w1 = [dscr("s_w1_%d" % l, 8) for l in range(2)]
    s_w2 = [dscr("s_w2_%d" % l, 8) for l in range(2)]

    with ExitStack() as st:
        E = st.enter_context
        xres_t = E(nc.sbuf_tensor("xres", [128, 8 * T], F32))
        arena_t = E(nc.sbuf_tensor("arena", [128, ARENA_KIB * 512], BF16))
        st32_t = E(nc.sbuf_tensor("st32", [128, 8 * 512], F32))
        stbf_t = E(nc.sbuf_tensor("stbf", [128, 2 * 8 * 512], BF16))
        wslot_t = [E(nc.sbuf_tensor("wslot%d" % i, [128, 4096], BF16)) for i in range(NSLOT)]
        tab_t = [E(nc.sbuf_tensor("tab%d" % i, [128, 2 * T], F32)) for i in range(NTAB)]
        rstd_t = E(nc.sbuf_tensor("rstd_sb", [128, T], F32))
        cols_t = E(nc.sbuf_tensor("cols_sb", [128, 88], F32))
        epsc_t = E(nc.sbuf_tensor("epsc", [128, 1], F32))
        rowA_t = E(nc.sbuf_tensor("rowA", [1, 2048], BF16))
        bsb_t = E(nc.sbuf_tensor("bsb", [128, 1024], F32))
        identf_t = E(nc.sbuf_tensor("identf", [128, 128], F32))
        identb_t = E(nc.sbuf_tensor("identb", [128, 128], BF16))
        ones_t = E(nc.sbuf_tensor("ones_sb", [128, 128], BF16))
        mask4_t = E(nc.sbuf_tensor("mask4", [128, 512], BF16))
        wsTf_t = E(nc.sbuf_tensor("wsTf", [128, 1024], F32))
        vss_t = E(nc.sbuf_tensor("vss", [128, 16], F32))
        vrs_t = E(nc.sbuf_tensor("vrs", [128, 4], F32))
        fss_t = E(nc.sbuf_tensor("fss", [128, 8], F32))
        frs_t = E(nc.sbuf_tensor("frs", [128, 4], F32))
        banks = [E(nc.psum_tensor("bank%d" % i, [128, 512], F32)) for i in range(8)]
        esem = {e: E(nc.semaphore("s_" + e)) for e in Prog.ENGS}
        ds_w = [DSem(E(nc.semaphore("d_w%d" % i))) for i in range(NSLOT)]
        ds_tab = [DSem(E(nc.semaphore("d_t%d" % i))) for i in range(NTAB)]
        ds_x = [DSem(E(nc.semaphore("d_x%d" % i))) for i in range(NCH)]
        ds_o = [DSem(E(nc.semaphore("d_o%d" % i))) for i in range(NCH)]
        ds_c = DSem(E(nc.semaphore("d_c")), group=True)
        ds_g = DSem(E(nc.semaphore("d_g")))
        cast_blocks = [("awin", 8), ("awout", 4), ("w1_0", 8), ("w2_0", 8), ("bwin", 12), ("bwout", 4), ("w1_1", 8), ("w2_1", 8)]
        ds_cast = {(nm, b): DSem(E(nc.semaphore("d_cast_%s_%d" % (nm, b))), group=True) for nm, nb_ in cast_blocks for b in range(nb_)}
        block = E(nc.Block())

        AR = Paged(arena_t, ARENA_KIB * 1024)
        K = 1024

        xres = [V(xres_t[:, k * T:(k + 1) * T], [Buf("xres%d" % k)]) for k in range(8)]
        wslot = [V(wslot_t[i][:], [Buf("wslot%d" % i)]) for i in range(NSLOT)]
        tabv = [V(tab_t[i][:], [Buf("tab%d" % i)]) for i in range(NTAB)]
        rstd = V(rstd_t[:], [Buf("rstd")])
        b_cols = Buf("cols")
        b_const = Buf("const")
        b_rows = Buf("rows")
        b_wsT = Buf("wsT")
        b_vss = Buf("vss")
        b_vrs = Buf("vrs")
        b_fss = Buf("fss")
        b_frs = [Buf("frs%d" % c) for c in range(NCH)]
        st32 = [[V(st32_t[:, (h * 2 + kc) * 512:(h * 2 + kc + 1) * 512], [Buf("st32_%d_%d" % (h, kc))]) for kc in range(2)] for h in range(4)]
        stbf2 = [[[V(stbf_t[:, (par * 8 + h * 2 + kc) * 512:(par * 8 + h * 2 + kc + 1) * 512], [Buf("stbf%d_%d_%d" % (par, h, kc))]) for kc in range(2)] for h in range(4)] for par in range(2)]
        pq = [[Buf("ps%d" % b)] for b in range(8)]

        def PS(b, q0=0, nq=4):
            return V(banks[b][:, q0 * 128:(q0 + nq) * 128], pq[b])

        def PSB(b):
            return banks[b][:].bitcast(BF16)

        bank_ctr = [0]

        def nb():
            b = bank_ctr[0] % 8
            bank_ctr[0] += 1
            return b

        COL_NMIX, COL_NFFN, COL_FIN, COL_BU, COL_GV, COL_HG = 0, 16, 32, 40, 56, 72

        def col(c):
            return cols_t[:, c:c + 1]

        def mm(out, lhsT_ap, rhs_ap, rd, start, stop):
            P.add("pe", lambda e: e.matmul(out.ap, lhsT=lhsT_ap, rhs=rhs_ap, start=start, stop=stop), reads=rd, writes=out.bufs)

        def tr(out_ap, out_bufs, in_ap, rd, ident_ap):
            P.add("pe", lambda e: e.transpose(out=out_ap, in_=in_ap, identity=ident_ap), reads=rd + [b_const], writes=out_bufs)

        def act(out, in_, func, rd=(), bias=None, scale=None, accum=None, extra_w=()):
            kw = {}
            if bias is not None:
                kw["bias"] = bias
            if scale is not None:
                kw["scale"] = scale
            if accum is not None:
                kw["accum_out"] = accum
            P.add("act", lambda e: e.activation(out=out.ap, in_=in_.ap, func=func, **kw), reads=list(in_.bufs) + list(rd), writes=list(out.bufs) + list(extra_w))

        def tt(eng, out, a, b, op):
            P.add(eng, lambda e: e.tensor_tensor(out=out.ap, in0=a.ap, in1=b.ap, op=op), reads=list(a.bufs) + list(b.bufs), writes=out.bufs)

        def stt(eng, out, a, scalar, b, op0, op1, rd=()):
            P.add(eng, lambda e: e.scalar_tensor_tensor(out=out.ap, in0=a.ap, scalar=scalar, in1=b.ap, op0=op0, op1=op1),
                  reads=list(a.bufs) + list(b.bufs) + list(rd), writes=out.bufs)

        def dma(q, out_ap, in_ap, rd, wr, dsem):
            P.add(q, lambda e: e.dma_start(out=out_ap, in_=in_ap), reads=rd, writes=wr, dsem=dsem)

        P.add("pool", lambda e: e.memset(identf_t[:], 1.0), writes=[b_const])
        P.add("pool", lambda e: e.affine_select(out=identf_t[:], in_=identf_t[:], pattern=[[-1, 128]], compare_op=ALU.is_equal,
                                                fill=0.0, base=0, channel_multiplier=1), reads=[b_const], writes=[b_const])
        P.add("pool", lambda e: e.tensor_copy(out=identb_t[:], in_=identf_t[:]), reads=[b_const], writes=[b_const])
        P.add("pool", lambda e: e.memset(ones_t[:], 1.0), writes=[b_const])
        P.add("pool", lambda e: e.memset(epsc_t[:], EPS), writes=[b_const])
        P.add("pool", lambda e: e.memset(mask4_t[:], 1.0), writes=[b_const])
        P.add("pool", lambda e: e.affine_select(out=mask4_t[:].rearrange("p (h i) -> p h i", h=4), in_=mask4_t[:].rearrange("p (h i) -> p h i", h=4),
                                                pattern=[[0, 4], [1, 128]], compare_op=ALU.is_ge, fill=0.0, base=0, channel_multiplier=-1),
              reads=[b_const], writes=[b_const])
        rowsf_ap = arena_t[0:1, 0:12288].bitcast(F32)
        rowsf_b = AR.pages[0:24]
        dma("sp", cols_t[:], cols_d, [], [b_cols], ds_c)
        dma("sp", rowsf_ap, rows_d.rearrange("(o r) n -> o (r n)", o=1), [], rowsf_b, ds_c)
        dma("sp", wsTf_t[:], wsT_d, [], [b_wsT], ds_c)
        b_bsb = Buf("bsb")
        dma("sp", bsb_t[:], rows_d[2:3, 0:1024].partition_broadcast(128), [], [b_bsb], ds_c)
        P.add("dve", lambda e: e.tensor_copy(out=rowA_t[:], in_=rowsf_ap[:, 0:2048]), reads=rowsf_b, writes=[b_rows])

        P.add("pool", lambda e: e.affine_select(out=wsTf_t[:].rearrange("p (g t) -> p g t", g=8), in_=wsTf_t[:].rearrange("p (g t) -> p g t", g=8),
                                                pattern=[[0, 8], [1, 128]], compare_op=ALU.is_ge, fill=0.0, base=0, channel_multiplier=-1),
              reads=[b_wsT], writes=[b_wsT])

        b_scr = {}
        cast_q = []

        def pump_casts(n):
            for _ in range(n):
                if cast_q:
                    o_, i_, w_, d_ = cast_q.pop(0)
                    dma("pool", o_, i_, [], w_, d_)

        def cast_type1(w_ap, scr, ncols, name, b0=0):
            src = w_ap.rearrange("(kc p) (b c) -> b p kc c", p=128, c=512)
            for b in range(ncols // 512):
                bb = Buf("%s_%d" % (name, b0 + b))
                b_scr[(name, b0 + b)] = [bb]
                cast_q.append((scr[b0 + b].rearrange("p (kc c) -> p kc c", c=512), src[b], [bb], ds_cast[(name, b0 + b)]))

        def cast_type2(w_ap, scr, kdim, cb, name):
            nk = kdim // 128
            src = w_ap.rearrange("(kc p) (b c) -> b p kc c", p=128, c=cb)
            for b in range(1024 // cb):
                bb = [Buf("%s_%d_0" % (name, b)), Buf("%s_%d_1" % (name, b))]
                b_scr[(name, b)] = bb
                dst = scr[b].rearrange("p (kc c) -> p kc c", c=cb)
                h = nk // 2
                cast_q.append((dst[:, 0:h, :], src[b][:, 0:h, :], [bb[0]], ds_cast[(name, b)]))
                cast_q.append((dst[:, h:nk, :], src[b][:, h:nk, :], [bb[1]], ds_cast[(name, b)]))

        cast_type1(a_w_in_d, s_awin, 4096, "awin")
        cast_type2(a_w_out_d, s_awout, 2048, 256, "awout")
        cast_type1(w1_d[0], s_w1[0], 4096, "w1_0")
        cast_type2(w2_d[0], s_w2[0], 4096, 128, "w2_0")
        cast_type1(b_w_in_d, s_bwin, 6144, "bwin")
        cast_type1(b_w_out_d[0:1024, :], s_bwout, 1024, "bwout", 0)
        cast_type1(b_w_out_d[1024:2048, :], s_bwout, 1024, "bwout", 2)
        cast_type1(w1_d[1], s_w1[1], 4096, "w1_1")
        cast_type2(w2_d[1], s_w2[1], 4096, 128, "w2_1")
        pump_casts(16)

        wctr = [0]

        def load_w(scr, name, b):
            i = wctr[0] % NSLOT
            wctr[0] += 1
            dma("sp", wslot[i].ap, scr[b], b_scr[(name, b)], wslot[i].bufs, ds_w[i])
            return wslot[i], wslot_t[i]

        xn_off = 0

        def xn(kc, c0=0, n=T):
            return AR.view(xn_off + (kc * T + c0) * 2, n, BF16)

        def rmsnorm_to_xn(colbase):
            P.stage = P.stage.split(":")[0] + ":norm%d" % colbase
            sqv = [AR.view(8 * K + kc * T * 2, T, BF16) for kc in range(8)]
            xg = [AR.view(64 * K + i * 2048, T, F32) for i in range(4)]
            for kc in range(8):
                if colbase == COL_NMIX and kc % 2 == 0:
                    tt("pool", sqv[kc], xres[kc], xres[kc], ALU.mult)
                else:
                    act(sqv[kc], xres[kc], AF.Square)
                if kc % 2 == 1:
                    act(xg[kc // 2], xres[kc], AF.Copy, rd=[b_cols], scale=col(colbase + kc))
            b = nb()
            for kc in range(8):
                mm(PS(b), ones_t[:], sqv[kc].ap, sqv[kc].bufs + [b_const], kc == 0, kc == 7)
            act(rstd, PS(b), AF.Sqrt, rd=[b_const], bias=epsc_t[:, 0:1], scale=1.0 / D)
            P.add("dve", lambda e: e.reciprocal(out=rstd.ap, in_=rstd.ap), reads=rstd.bufs, writes=rstd.bufs)
            for kc in range(8):
                if kc % 2 == 0:
                    stt("dve", xn(kc), xres[kc], col(colbase + kc), rstd, ALU.mult, ALU.mult, rd=[b_cols])
                else:
                    tt("pool", xn(kc), xg[kc // 2], rstd, ALU.mult)

        def load_x_tile(tok0):
            xin = [AR.view(16 * K + c * 4096, 1024, F32) for c in range(NCH)]
            for c in range(NCH):
                dma("sp", xin[c].ap, x_d[tok0 + c * 128: tok0 + (c + 1) * 128, :], [], xin[c].bufs, ds_x[c])
            for kc in range(8):
                b = nb()
                for c in range(NCH):
                    o = PS(b, c, 1)
                    tr(o.ap, o.bufs, xin[c].ap[:, kc * 128:(kc + 1) * 128], xin[c].bufs, identf_t[:])
                if kc % 2 == 0:
                    act(xres[kc], PS(b), AF.Copy)
                else:
                    P.add("dve", (lambda kc, b: lambda e: e.tensor_copy(out=xres[kc].ap, in_=banks[b][:]))(kc, b), reads=pq[b], writes=xres[kc].bufs)

        xnext = [AR.view(72 * K + kc * 2048, T, F32) for kc in range(8)]

        def prefetch_units(tok0n):
            xin = [AR.view(16 * K + c * 4096, 1024, F32) for c in range(NCH)]
            sqv = [AR.view(8 * K + kc * T * 2, T, BF16) for kc in range(8)]
            xg = [AR.view(64 * K + i * 2048, T, F32) for i in range(4)]
            colbase = COL_NMIX + 0

            def unit(kc):
                def f():
                    if kc == 0:
                        for c in range(NCH):
                            dma("sp", xin[c].ap, x_d[tok0n + c * 128: tok0n + (c + 1) * 128, :], [], xin[c].bufs, ds_x[c])
                    b = nb()
                    for c in range(NCH):
                        o = PS(b, c, 1)
                        tr(o.ap, o.bufs, xin[c].ap[:, kc * 128:(kc + 1) * 128], xin[c].bufs, identf_t[:])
                    if kc % 2 == 0:
                        act(xnext[kc], PS(b), AF.Copy)
                    else:
                        P.add("dve", (lambda kc, b: lambda e: e.tensor_copy(out=xnext[kc].ap, in_=banks[b][:]))(kc, b), reads=pq[b], writes=xnext[kc].bufs)
                    if kc % 2 == 0:
                        tt("pool", sqv[kc], xnext[kc], xnext[kc], ALU.mult)
                    else:
                        act(sqv[kc], xnext[kc], AF.Square)
                        act(xg[kc // 2], xnext[kc], AF.Copy, rd=[b_cols], scale=col(colbase + kc))
                    if kc == 7:
                        b2 = nb()
                        for k2 in range(8):
                            mm(PS(b2), ones_t[:], sqv[k2].ap, sqv[k2].bufs + [b_const], k2 == 0, k2 == 7)
                        act(rstd, PS(b2), AF.Sqrt, rd=[b_const], bias=epsc_t[:, 0:1], scale=1.0 / D)
                        P.add("dve", lambda e: e.reciprocal(out=rstd.ap, in_=rstd.ap), reads=rstd.bufs, writes=rstd.bufs)
                        for k2 in range(8):
                            if k2 % 2 == 0:
                                stt("dve", xn(k2), xnext[k2], col(colbase + k2), rstd, ALU.mult, ALU.mult, rd=[b_cols])
                            else:
                                tt("pool", xn(k2), xg[k2 // 2], rstd, ALU.mult)
                return f
            return [unit(kc) for kc in range(8)]

        def adopt_next():
            for kc in range(8):
                eng = ("act", "dve", "pool")[kc % 3]
                if eng == "act":
                    act(xres[kc], xnext[kc], AF.Copy)
                else:
                    P.add(eng, (lambda kc: lambda e: e.tensor_copy(out=xres[kc].ap, in_=xnext[kc].ap))(kc), reads=xnext[kc].bufs, writes=xres[kc].bufs)

        def mixer_a(prefetched=False):
            if not prefetched:
                rmsnorm_to_xn(COL_NMIX + 0)
            pump_casts(36)
            u = [AR.view(16 * K + fc * T * 2, T, BF16) for fc in range(16)]
            vbf = [AR.view(32 * K + c * 4096, 2048, BF16) for c in range(NCH)]
            wTs = [AR.view(48 * K + i * 2048, 1024, BF16) for i in range(4)]
            junk = AR.view(56 * K, 512, BF16)
            sgt = [AR.view(58 * K + i * 1024, 128, F32) for i in range(4)]
            P.stage = P.stage.split(":")[0] + ":A_v"
            b_vssc = [Buf("vss%d" % c) for c in range(NCH)]
            b_vrsc = [Buf("vrs%d" % c) for c in range(NCH)]
            P.add("pool", lambda e: e.memset(vss_t[:], 0.0), writes=b_vssc)
            for vb in range(4):
                ws, wt = load_w(s_awin, "awin", 4 + vb)
                for c in range(NCH):
                    b = nb()
                    mm(PS(b), ones_t[0:1, :], rowA_t[0:1, vb * 512:(vb + 1) * 512], [b_const, b_rows], True, False)
                    for kc in range(8):
                        xv = xn(kc, c * 128, 128)
                        mm(PS(b), xv.ap, wt[:, kc * 512:(kc + 1) * 512], ws.bufs + xv.bufs, False, kc == 7)
                    vv = V(vbf[c].ap[:, vb * 512:(vb + 1) * 512], vbf[c].bufs)
                    act(vv, PS(b), AF.Gelu_apprx_tanh)
                    act(junk, vv, AF.Square, accum=vss_t[:, c * 4 + vb: c * 4 + vb + 1], extra_w=[b_vssc[c]])
            for c in range(NCH):
                P.add("dve", (lambda c: lambda e: e.tensor_reduce(out=vrs_t[:, c:c + 1], in_=vss_t[:, c * 4:(c + 1) * 4], axis=mybir.AxisListType.X, op=ALU.add))(c),
                      reads=[b_vssc[c]], writes=[b_vrsc[c]])
                P.add("act", (lambda c: lambda e: e.activation(out=vrs_t[:, c:c + 1], in_=vrs_t[:, c:c + 1], func=AF.Sqrt, bias=epsc_t[:, 0:1], scale=1.0 / 2048))(c),
                      reads=[b_vrsc[c], b_const], writes=[b_vrsc[c]])
                P.add("dve", (lambda c: lambda e: e.reciprocal(out=vrs_t[:, c:c + 1], in_=vrs_t[:, c:c + 1]))(c), reads=[b_vrsc[c]], writes=[b_vrsc[c]])
                P.add("act", (lambda w_c, c: lambda e: e.activation(out=w_c.ap, in_=wsTf_t[:], func=AF.Copy, scale=vrs_t[:, c:c + 1]))(wTs[c], c),
                      reads=[b_wsT, b_vrsc[c]], writes=wTs[c].bufs)

            def sgate_group(gq):
                for c in range(NCH):
                    w_c = wTs[c]
                    b = nb()
                    for r in range(4):
                        fc = gq * 4 + r
                        g = fc // 2
                        o = PS(b, r, 1)
                        mm(o, vbf[c].ap[:, fc * 128:(fc + 1) * 128], w_c.ap[:, g * 128:(g + 1) * 128], vbf[c].bufs + w_c.bufs, True, True)
                    for r in range(4):
                        fc = gq * 4 + r
                        g = fc // 2
                        o = PS(b, r, 1)
                        uu = V(u[fc].ap[:, c * 128:(c + 1) * 128], u[fc].bufs)
                        st_ = sgt[r]
                        stt("dve", st_, o, col(COL_GV + fc), V(bsb_t[:, g * 128:(g + 1) * 128], [b_bsb]), ALU.mult, ALU.add, rd=[b_cols])
                        tt("pool", uu, st_, uu, ALU.mult)

            for blk in range(4):
                P.stage = P.stage.split(":")[0] + ":A_u"
                ws, wt = load_w(s_awin, "awin", blk)
                for m in range(4):
                    b = nb()
                    for kc in range(8):
                        mm(PS(b), wt[:, kc * 512 + m * 128: kc * 512 + (m + 1) * 128], xn(kc).ap, ws.bufs + xn(kc).bufs, kc == 0, kc == 7)
                    fc = blk * 4 + m
                    act(u[fc], PS(b), AF.Gelu_apprx_tanh, rd=[b_cols], bias=col(COL_BU + fc))
                if blk >= 1:
                    P.stage = P.stage.split(":")[0] + ":A_sg"
                    sgate_group(blk - 1)
            P.stage = P.stage.split(":")[0] + ":A_sg"
            sgate_group(3)
            P.stage = P.stage.split(":")[0] + ":A_out"
            w01 = [load_w(s_awout, "awout", 0), load_w(s_awout, "awout", 1)]
            b01 = [nb() for _ in range(4)]
            for lo, hi in ((0, 12), (12, 16)):
                for m in range(4):
                    ws, wt = w01[m // 2]
                    mmi = m % 2
                    for kc in range(lo, hi):
                        mm(PS(b01[m]), wt[:, kc * 256 + mmi * 128: kc * 256 + (mmi + 1) * 128], u[kc].ap, ws.bufs + u[kc].bufs, kc == 0, kc == 15)
                    if hi == 16:
                        tt("dve", xres[m], xres[m], PS(b01[m]), ALU.add)
            for blk in range(2, 4):
                ws, wt = load_w(s_awout, "awout", blk)
                for mmi in range(2):
                    m = blk * 2 + mmi
                    b = nb()
                    for kc in range(16):
                        mm(PS(b), wt[:, kc * 256 + mmi * 128: kc * 256 + (mmi + 1) * 128], u[kc].ap, ws.bufs + u[kc].bufs, kc == 0, kc == 15)
                    tt("dve", xres[m], xres[m], PS(b), ALU.add)

        def mlp(l, fillers=()):
            colbase = COL_NFFN + 8 * l
            P.stage = P.stage.split(":")[0] + ":norm%d" % colbase
            sqv = [AR.view(8 * K + kc * T * 2, T, BF16) for kc in range(8)]
            for kc in range(8):
                act(xn(kc), xres[kc], AF.Copy, rd=[b_cols], scale=col(colbase + kc))
            for kc in range(8):
                act(sqv[kc], xres[kc], AF.Square)
            b = nb()
            for kc in range(8):
                mm(PS(b), ones_t[:], sqv[kc].ap, sqv[kc].bufs + [b_const], kc == 0, kc == 7)
            act(rstd, PS(b), AF.Sqrt, rd=[b_const], bias=epsc_t[:, 0:1], scale=1.0 / D)
            P.add("dve", lambda e: e.reciprocal(out=rstd.ap, in_=rstd.ap), reads=rstd.bufs, writes=rstd.bufs)
            H = [AR.view(32 * K + fc * T * 2, T, BF16) for fc in range(32)]
            rt = [AR.view(64 * K + i * 2048, T, F32) for i in range(4)]
            ri = 0
            P.stage = P.stage.split(":")[0] + ":m%d_w1" % l
            for blk in range(8):
                ws, wt = load_w(s_w1[l], "w1_%d" % l, blk)
                for m in range(4):
                    b = nb()
                    for kc in range(8):
                        mm(PS(b), wt[:, kc * 512 + m * 128: kc * 512 + (m + 1) * 128], xn(kc).ap, ws.bufs + xn(kc).bufs, kc == 0, kc == 7)
                    r = rt[ri % 4]
                    ri += 1
                    stt("dve", r, PS(b), 0.0, rstd, ALU.max, ALU.mult)
                    tt("pool", H[blk * 4 + m], r, r, ALU.mult)
                    if blk >= 1:
                        pump_casts(2)
            P.stage = P.stage.split(":")[0] + ":m%d_w2" % l
            for m in range(8):
                ws, wt = load_w(s_w2[l], "w2_%d" % l, m)
                b = nb()
                for kc in range(32):
                    mm(PS(b), wt[:, kc * 128:(kc + 1) * 128], H[kc].ap, ws.bufs + H[kc].bufs, kc == 0, kc == 31)
                tt("dve", xres[m], xres[m], PS(b), ALU.add)
                if m < len(fillers):
                    fillers[m]()

        tabctr = [0]

        def mixer_b(tile_idx):
            rmsnorm_to_xn(COL_NMIX + 8)
            qT = [AR.view(16 * K + i * T * 2, T, BF16) for i in range(8)]
            kT = [AR.view(24 * K + i * T * 2, T, BF16) for i in range(8)]
            ktok = [AR.view(32 * K + c * 2048, 1024, BF16) for c in range(NCH)]
            vtok = [AR.view(40 * K + c * 4096, 2048, BF16) for c in range(NCH)]
            sg = [AR.view(56 * K + fc * T * 2, T, BF16) for fc in range(16)]
            rtmp = [AR.view(64 * K + i * 2048, T, F32) for i in range(4)]
            yoff = [8 * K + i * 1024 for i in range(8)] + [16 * K + i * 1024 for i in range(4)] + [24 * K + i * 1024 for i in range(4)]
            yT = [AR.view(yoff[fc], T, BF16) for fc in range(16)]
            PT = [AR.view(72 * K + i * 1024, 512, BF16) for i in range(2)]
            sqh = [AR.view(74 * K + i * 1024, 512, BF16) for i in range(2)]
            rs = [AR.view(76 * K + i * 2048, 512, F32) for i in range(2)]
            y1h = [AR.view(80 * K + i * 2048, 512, F32) for i in range(2)]
            sgtmp = [AR.view(84 * K + i * 2048, 512, F32) for i in range(2)]
            fbank = [0]
            ubank = [0]

            def bank_free():
                return nb()

            def bank_fill():
                fbank[0] ^= 1
                return 6 + fbank[0]

            def rotary(isk, h, b1, b2):
                dst = kT if isk else qT
                ti = tabctr[0] % NTAB
                tabctr[0] += 1
                dma("pool", tabv[ti].ap, tabs_d[tile_idx, isk * 4 + h], [], tabv[ti].bufs, ds_tab[ti])
                Cd = V(tab_t[ti][:, 0:T], tabv[ti].bufs)
                Sd = V(tab_t[ti][:, T:2 * T], tabv[ti].bufs)
                t1 = PS(b1)
                t2 = PS(b2)
                tt("dve", rtmp[0], t1, Cd, ALU.mult)
                tt("dve", rtmp[1], t2, Sd, ALU.mult)
                tt("pool", dst[h * 2 + 0], rtmp[0], rtmp[1], ALU.subtract)
                tt("dve", rtmp[2], t2, Cd, ALU.mult)
                tt("dve", rtmp[3], t1, Sd, ALU.mult)
                tt("pool", dst[h * 2 + 1], rtmp[2], rtmp[3], ALU.add)

            def units_qk(blk, balloc):
                st_ = {}

                def unit(m):
                    def f():
                        if m == 0:
                            st_["w"] = load_w(s_bwin, "bwin", blk)
                            st_["b"] = []
                        ws, wt = st_["w"]
                        b = balloc()
                        st_["b"].append(b)
                        for kc in range(8):
                            mm(PS(b), wt[:, kc * 512 + m * 128: kc * 512 + (m + 1) * 128], xn(kc).ap, ws.bufs + xn(kc).bufs, kc == 0, kc == 7)
                        if m % 2 == 1:
                            rotary(blk // 2, (blk % 2) * 2 + m // 2, st_["b"][m - 1], st_["b"][m])
                    return f
                return [unit(m) for m in range(4)]

            def units_v(vb, balloc):
                st_ = {}

                def unit(c):
                    def f():
                        if c == 0:
                            st_["w"] = load_w(s_bwin, "bwin", 4 + vb)
                        ws, wt = st_["w"]
                        b = balloc()
                        for kc in range(8):
                            xv = xn(kc, c * 128, 128)
                            mm(PS(b), xv.ap, wt[:, kc * 512:(kc + 1) * 512], ws.bufs + xv.bufs, kc == 0, kc == 7)
                        act(V(vtok[c].ap[:, vb * 512:(vb + 1) * 512], vtok[c].bufs), PS(b), AF.Copy)
                    return f
                return [unit(c) for c in range(NCH)]

            def units_g(gb, balloc):
                st_ = {}

                def unit(m):
                    def f():
                        if m == 0:
                            st_["w"] = load_w(s_bwin, "bwin", 8 + gb)
                        ws, wt = st_["w"]
                        b = balloc()
                        for kc in range(8):
                            mm(PS(b), wt[:, kc * 512 + m * 128: kc * 512 + (m + 1) * 128], xn(kc).ap, ws.bufs + xn(kc).bufs, kc == 0, kc == 7)
                        fc = gb * 4 + m
                        tmp_ = sgtmp[fc % 2]
                        act(tmp_, PS(b), AF.Silu)
                        act(sg[fc], tmp_, AF.Copy, rd=[b_cols], scale=col(COL_HG + fc))
                    return f
                return [unit(m) for m in range(4)]

            def transposes(pair, balloc):
                for c in range(NCH):
                    b = balloc()
                    pb = PSB(b)
                    for j in range(4):
                        i = pair * 4 + j
                        tr(pb[:, j * 128:(j + 1) * 128], pq[b], kT[i].ap[:, c * 128:(c + 1) * 128], kT[i].bufs, identb_t[:])
                    P.add("act", (lambda kt, pb, pair: lambda e: e.activation(out=kt.ap[:, pair * 512:(pair + 1) * 512], in_=pb[:, 0:512], func=AF.Copy))(ktok[c], pb, pair),
                          reads=pq[b], writes=ktok[c].bufs)

            def st_S(pair, c):
                cs = slice(c * 128, (c + 1) * 128)
                for s_ in range(2):
                    h = pair * 2 + s_
                    for half in range(2):
                        i = h * 2 + half
                        mm(PS(0, s_, 1), kT[i].ap[:, cs], qT[i].ap[:, cs], kT[i].bufs + qT[i].bufs, half == 0, half == 1)
                pt = PT[c % 2]
                P.add("dve", (lambda pt: lambda e: e.tensor_tensor(out=pt.ap[:, 0:256], in0=banks[0][:, 0:256], in1=mask4_t[:, 0:256], op=ALU.mult))(pt),
                      reads=pq[0] + [b_const], writes=pt.bufs)

            def st_O(pair, c, s_):
                h = pair * 2 + s_
                cs = slice(c * 128, (c + 1) * 128)
                pt = PT[c % 2]
                b = 4 + s_
                for vc in range(4):
                    o = PS(b, vc, 1)
                    mm(o, vtok[c].ap[:, h * 512 + vc * 128: h * 512 + (vc + 1) * 128], pt.ap[:, s_ * 128:(s_ + 1) * 128], vtok[c].bufs + pt.bufs, True, False)
                    for kc in range(2):
                        sb_ = stbf2[c % 2][h][kc]
                        mm(o, sb_.ap[:, vc * 128:(vc + 1) * 128], qT[h * 2 + kc].ap[:, cs], sb_.bufs + qT[h * 2 + kc].bufs, False, kc == 1)
                act(sqh[s_], PS(b), AF.Square)

            def st_N(pair, c, s_):
                sq_ = sqh[s_]
                for vc in range(4):
                    mm(PS(0, 2 + s_, 1), ones_t[:], sq_.ap[:, vc * 128:(vc + 1) * 128], sq_.bufs + [b_const], vc == 0, vc == 3)
                if s_ == 0:
                    return
                r2 = V(rs[c % 2].ap[:, 0:256], rs[c % 2].bufs)
                act(r2, PS(0, 2, 2), AF.Sqrt, rd=[b_const], bias=epsc_t[:, 0:1], scale=1.0 / 512)
                P.add("dve", (lambda r2: lambda e: e.reciprocal(out=r2.ap, in_=r2.ap))(r2), reads=r2.bufs, writes=r2.bufs)
                for ss in range(2):
                    h = pair * 2 + ss
                    b = 4 + ss
                    yh = y1h[ss]
                    y4 = yh.ap.rearrange("p (v i) -> p v i", v=4)
                    o4 = banks[b][:].rearrange("p (v i) -> p v i", v=4)
                    rb = r2.ap[:, ss * 128:(ss + 1) * 128].unsqueeze(1).to_broadcast([128, 4, 128])
                    P.add("dve", (lambda y4, o4, rb: lambda e: e.tensor_tensor(out=y4, in0=o4, in1=rb, op=ALU.mult))(y4, o4, rb), reads=pq[b] + r2.bufs, writes=yh.bufs)
                    fc0 = h * 4
                    sg4 = arena_t[:, (56 * K + fc0 * 1024) // 2:(56 * K + (fc0 + 4) * 1024) // 2].rearrange("p (v t) -> p v t", v=4)[:, :, c * 128:(c + 1) * 128]
                    yo4 = arena_t[:, yoff[fc0] // 2:(yoff[fc0] + 4 * 1024) // 2].rearrange("p (v t) -> p v t", v=4)[:, :, c * 128:(c + 1) * 128]
                    sgb = [bb for fc in range(fc0, fc0 + 4) for bb in sg[fc].bufs]
                    yob = [bb for fc in range(fc0, fc0 + 4) for bb in yT[fc].bufs]
                    P.add("pool", (lambda yo4, y4, sg4: lambda e: e.tensor_tensor(out=yo4, in0=y4, in1=sg4, op=ALU.mult))(yo4, y4, sg4), reads=yh.bufs + sgb, writes=yob)

            def st_U(pair, c, s_):
                h = pair * 2 + s_
                gC = float(np.float32(1.0 - 2.0 ** (-5.0 - h)) ** 128)
                for kc in range(2):
                    bd = 1 + (ubank[0] % 3)
                    ubank[0] += 1
                    mm(PS(bd), ktok[c].ap[:, h * 256 + kc * 128: h * 256 + (kc + 1) * 128], vtok[c].ap[:, h * 512:(h + 1) * 512], ktok[c].bufs + vtok[c].bufs, True, True)
                    s32 = st32[h][kc]
                    P.add("dve", (lambda s32, bd, gC: lambda e: e.scalar_tensor_tensor(out=s32.ap, in0=s32.ap, scalar=gC, in1=banks[bd][:], op0=ALU.mult, op1=ALU.add))(s32, bd, gC),
                          reads=s32.bufs + pq[bd], writes=s32.bufs)
                    act(stbf2[(c + 1) % 2][h][kc], s32, AF.Copy, scale=gC)

            def retention(pair, fillers, front=0):
                steps = [lambda: st_S(pair, 0)]
                for c in range(NCH):
                    steps.append((lambda c: lambda: st_U(pair, c, 0))(c))
                    steps.append((lambda c: lambda: st_O(pair, c, 0))(c))
                    steps.append((lambda c: lambda: st_U(pair, c, 1))(c))
                    steps.append((lambda c: lambda: st_O(pair, c, 1))(c))
                    if c + 1 < NCH:
                        steps.append((lambda c: lambda: st_S(pair, c + 1))(c))
                    steps.append((lambda c: lambda: st_N(pair, c, 0))(c))
                    steps.append((lambda c: lambda: st_N(pair, c, 1))(c))
                nf = len(fillers)
                done = 0
                for i, stp in enumerate(steps):
                    stp()
                    if front:
                        want = min(front, 2 * (i + 1)) if 2 * (i + 1) <= front + 1 else front + ((i + 1 - front // 2) * (nf - front) + len(steps) - front // 2 - 1) // (len(steps) - front // 2)
                    else:
                        want = ((i + 1) * nf + len(steps) - 1) // len(steps)
                    while done < min(want, nf):
                        fillers[done]()
                        done += 1
                while done < nf:
                    fillers[done]()
                    done += 1

            def units_wo(half, balloc):
                us = []
                for j in range(2):
                    st_ = {}

                    def unit(j, mmi, st_=st_):
                        def f():
                            if mmi == 0:
                                st_["w"] = load_w(s_bwout, "bwout", half * 2 + j)
                            ws, wt = st_["w"]
                            m = j * 4 + mmi
                            b = balloc()
                            for kc in range(8):
                                yk = yT[half * 8 + kc]
                                mm(PS(b), wt[:, kc * 512 + mmi * 128: kc * 512 + (mmi + 1) * 128], yk.ap, ws.bufs + yk.bufs, kc == 0, kc == 7)
                            tt("dve", xres[m], xres[m], PS(b), ALU.add)
                        return f
                    us += [unit(j, mmi) for mmi in range(4)]
                return us

            P.stage = P.stage.split(":")[0] + ":B_pre"
            for u_ in units_qk(0, bank_free) + units_qk(2, bank_free) + units_v(0, bank_free) + units_v(1, bank_free) + units_g(0, bank_free) + units_g(1, bank_free):
                u_()
            P.stage = P.stage.split(":")[0] + ":B_retA"
            transposes(0, bank_free)
            retention(0, units_qk(1, bank_fill) + units_qk(3, bank_fill) + units_v(2, bank_fill) + units_v(3, bank_fill))
            P.stage = P.stage.split(":")[0] + ":B_retB"
            transposes(1, bank_fill)
            retention(1, units_g(2, bank_fill) + units_g(3, bank_fill) + units_wo(0, bank_fill), front=8)
            P.stage = P.stage.split(":")[0] + ":B_out2"
            for u_ in units_wo(1, bank_free):
                u_()

        def store_tile(tok0, do_norm):
            P.stage = P.stage.split(":")[0] + ":fin"
            ost = [AR.view(48 * K + c * 4096, 1024, F32) for c in range(NCH)]
            if do_norm:
                gbc = AR.view(32 * K, 1024, F32)
                fjunk = AR.view(36 * K, 512, BF16)
                dma("pool", gbc.ap, rows_d[2:3, 1024:2048].partition_broadcast(128), [], gbc.bufs, ds_g)
                P.add("pool", lambda e: e.memset(fss_t[:], 0.0), writes=[b_fss])
            for c in range(NCH):
                bh = [nb(), nb()]
                for half in range(2):
                    for j in range(4):
                        kc = half * 4 + j
                        o = PS(bh[half], j, 1)
                        tr(o.ap, o.bufs, xres[kc].ap[:, c * 128:(c + 1) * 128], xres[kc].bufs, identf_t[:])
                if do_norm:
                    for half in range(2):
                        act(fjunk, PS(bh[half]), AF.Square, accum=fss_t[:, c * 2 + half: c * 2 + half + 1], extra_w=[b_fss])
                    P.add("dve", (lambda c: lambda e: e.tensor_reduce(out=frs_t[:, c:c + 1], in_=fss_t[:, c * 2:c * 2 + 2], axis=mybir.AxisListType.X, op=ALU.add))(c),
                          reads=[b_fss], writes=[b_frs[c]])
                    P.add("act", (lambda c: lambda e: e.activation(out=frs_t[:, c:c + 1], in_=frs_t[:, c:c + 1], func=AF.Sqrt, bias=epsc_t[:, 0:1], scale=1.0 / D))(c),
                          reads=[b_frs[c], b_const], writes=[b_frs[c]])
                    P.add("dve", (lambda c: lambda e: e.reciprocal(out=frs_t[:, c:c + 1], in_=frs_t[:, c:c + 1]))(c), reads=[b_frs[c]], writes=[b_frs[c]])
                    for half in range(2):
                        stt("dve", V(ost[c].ap[:, half * 512:(half + 1) * 512], ost[c].bufs), PS(bh[half]), frs_t[:, c:c + 1],
                            V(gbc.ap[:, half * 512:(half + 1) * 512], gbc.bufs), ALU.mult, ALU.mult, rd=[b_frs[c]])
                else:
                    for half in range(2):
                        act(V(ost[c].ap[:, half * 512:(half + 1) * 512], ost[c].bufs), PS(bh[half]), AF.Copy)
                dma("act", out_d[tok0 + c * 128: tok0 + (c + 1) * 128, :], ost[c].ap, ost[c].bufs, [], ds_o[c])

        for s in range(nseq):
            for h in range(4):
                for kc in range(2):
                    P.add("pool", (lambda v: lambda e: e.memset(v.ap, 0.0))(st32[h][kc]), writes=st32[h][kc].bufs)
                    P.add("pool", (lambda v: lambda e: e.memset(v.ap, 0.0))(stbf2[0][h][kc]), writes=stbf2[0][h][kc].bufs)
            for ti in range(ntile):
                tok0 = s * seqlen + ti * T
                gi = s * ntile + ti
                P.stage = "t%d:ldx" % gi
                pref = (dbg_stop is None) and gi > 0
                if pref:
                    adopt_next()
                else:
                    load_x_tile(tok0)
                if dbg_stop == "load":
                    store_tile(tok0, False)
                    continue
                mixer_a(prefetched=pref)
                if dbg_stop == "mixa":
                    store_tile(tok0, False)
                    continue
                mlp(0)
                if dbg_stop == "mlp0":
                    store_tile(tok0, False)
                    continue
                mixer_b(ti)
                if dbg_stop == "mixb":
                    store_tile(tok0, False)
                    continue
                has_next = (dbg_stop is None) and (gi + 1 < nseq * ntile)
                mlp(1, fillers=prefetch_units((gi + 1) * T) if has_next else ())
                if dbg_stop == "mlp1":
                    store_tile(tok0, False)
                    continue
                store_tile(tok0, True)

        counts = P.emit(nc, block, esem)
        import os
        if os.environ.get("MK_STAGE_LOG"):
            import json
            json.dump(P.stage_log, open(os.environ["MK_STAGE_LOG"], "w"))
    return nc, counts


def _tables(seqlen):
    f32 = np.float32
    pos = np.arange(seqlen, dtype=f32)
    inv_freq = (1.0 / (f32(10000.0) ** np.linspace(0.0, 1.0, 128, dtype=f32))).astype(f32)
    ang = (pos[:, None] * inv_freq[None, :]).astype(f32)
    cos = np.cos(ang).astype(f32).T
    sin = np.sin(ang).astype(f32).T
    idx = (np.arange(seqlen) % 128).astype(f32)
    ntile = seqlen // T
    tabs = np.zeros((ntile, 8, 128, 2 * T), dtype=f32)
    for h in range(4):
        lg = np.log(f32(1.0) - f32(2.0) ** f32(-5.0 - h)).astype(f32)
        qd = np.exp(lg * (idx + 1.0)).astype(f32)
        kd = (np.exp(-lg * (idx + 1.0)) * f32(256.0 ** -0.5)).astype(f32)
        for isk, dd in ((0, qd), (1, kd)):
            C = (cos * dd[None, :]).astype(f32).reshape(128, ntile, T)
            S_ = (sin * dd[None, :]).astype(f32).reshape(128, ntile, T)
            tabs[:, isk * 4 + h, :, 0:T] = C.transpose(1, 0, 2)
            tabs[:, isk * 4 + h, :, T:2 * T] = S_.transpose(1, 0, 2)
    return tabs


def _colize(v):
    v = np.asarray(v, dtype=np.float32)
    return v.reshape(-1, 128).T


def _prep_shared(inp, seqlen):
    cols = np.concatenate([
        _colize(inp["norm_mix_g"][0]), _colize(inp["norm_mix_g"][1]),
        _colize(inp["norm_ffn_g"][0]), _colize(inp["norm_ffn_g"][1]),
        _colize(inp["final_norm_g"]),
        _colize(inp["a_b_in"][0][:2048]),
        _colize(inp["a_v_norm_g"][0]),
        _colize(inp["b_head_norm_g"][0]),
    ], axis=1).astype(np.float32)
    assert cols.shape == (128, 88)
    rows = np.zeros((3, 2048), dtype=np.float32)
    rows[0] = inp["a_b_in"][0][2048:]
    rows[1] = inp["a_v_norm_g"][0]
    rows[2, :1024] = np.asarray(inp["a_b_s"][0]).reshape(-1)
    rows[2, 1024:] = np.asarray(inp["final_norm_g"]).reshape(-1)
    wsT = np.ascontiguousarray(np.asarray(inp["a_w_s"][0]).transpose(2, 0, 1)).reshape(128, 1024)
    shared = {
        "a_w_in": np.ascontiguousarray(inp["a_w_in"][0]),
        "a_w_out": np.ascontiguousarray(inp["a_w_out"][0]),
        "b_w_in": np.ascontiguousarray(inp["b_w_in"][0]),
        "b_w_out": np.ascontiguousarray(inp["b_w_out"][0]),
        "mlp_w1": np.ascontiguousarray(inp["mlp_w1"]),
        "mlp_w2": np.ascontiguousarray(inp["mlp_w2"]),
        "cols": np.ascontiguousarray(cols),
        "rows": rows,
        "wsT": wsT,
        "tabs": _tables(seqlen),
    }
    return shared


_CACHE = {}


def run(inp, ncores, nseq, seqlen, dbg_stop=None):
    inp = {k: np.asarray(v, dtype=np.float32) for k, v in inp.items()}
    key = (nseq, seqlen, dbg_stop)
    if key not in _CACHE:
        _CACHE[key] = build_program(nseq, seqlen, dbg_stop)
    nc, counts = _CACHE[key]
    shared = _prep_shared(inp, seqlen)
    x = inp["x"]
    in_maps = []
    for i in range(ncores):
        m = dict(shared)
        m["x"] = np.ascontiguousarray(x[i * nseq:(i + 1) * nseq].reshape(nseq * seqlen, D))
        in_maps.append(m)
    res = run_bass_kernel_spmd(nc, in_maps, core_ids=list(range(ncores)))
    outs = [np.asarray(r["out"]).reshape(nseq, seqlen, D) for r in res.results]
    return np.concatenate(outs, axis=0).astype(np.float32)


def kernel(**inputs):
    return run(inputs, 8, 2, 4096)
```
